# Optimizing a Trainium2 kernel written in Bass

```python
import math
import jax, jax.numpy as jnp
from jax import lax
import numpy as np

D_MODEL = 1024
BATCH = 4
SEQ = 4096
DEPTH = 1

CHUNK = 64
MIX_WIDTH = D_MODEL
SB_WIDTH = D_MODEL // 2
SB_HEADS = 8
SB_HEAD_DIM = SB_WIDTH // SB_HEADS
SB_Q_BLOCK = 128
GLA_WIDTH = MIX_WIDTH - SB_WIDTH
GLA_HEADS = 4
GLA_KEY_DIM = GLA_WIDTH // 2 // GLA_HEADS
GLA_VAL_DIM = GLA_WIDTH // GLA_HEADS
GLA_GATE_RANK = 16
GLA_TAU = 16.0
D_FF = 2816
CONV_WIDTH = 3
LN_EPS = 1e-5
RMS_EPS = 1e-6
DN_ALPHA = (2.0 * DEPTH) ** 0.25
DN_BETA = (8.0 * DEPTH) ** -0.25

IN_SIZES = (SB_WIDTH, SB_WIDTH, SB_WIDTH,
            GLA_HEADS * GLA_KEY_DIM, GLA_HEADS * GLA_KEY_DIM, GLA_WIDTH, GLA_WIDTH,
            GLA_GATE_RANK)
IN_SPLITS = tuple(int(v) for v in np.cumsum(IN_SIZES)[:-1])
IN_WIDTH = int(sum(IN_SIZES))

kernel_name = "hybrid_stickbreak_gla_convffn_deepnorm"


def layer_norm(x, g, b):
    xf = x.astype(jnp.float32)
    mu = jnp.mean(xf, axis=-1, keepdims=True)
    var = jnp.mean(jnp.square(xf - mu), axis=-1, keepdims=True)
    y = (xf - mu) * lax.rsqrt(var + LN_EPS) * g.astype(jnp.float32) + b.astype(jnp.float32)
    return y.astype(x.dtype)


def stick_breaking_attention(q, k, v):
    S = q.shape[2]
    scale = SB_HEAD_DIM ** -0.5
    outs = []
    for q0 in range(0, S, SB_Q_BLOCK):
        L = q0 + SB_Q_BLOCK
        qb = q[:, :, q0:L].astype(jnp.float32)
        kb = k[:, :, :L].astype(jnp.float32)
        vb = v[:, :, :L].astype(jnp.float32)
        z = jnp.einsum("bhqd,bhkd->bhqk", qb, kb) * scale
        qpos = q0 + jnp.arange(SB_Q_BLOCK)[:, None]
        kpos = jnp.arange(L)[None, :]
        strict = kpos < qpos
        log_1m = jnp.where(strict, jax.nn.log_sigmoid(-z), 0.0)
        suffix = lax.cumsum(log_1m, axis=3, reverse=True) - log_1m
        log_w = jax.nn.log_sigmoid(z) + suffix
        w = jnp.where(strict, jnp.exp(log_w), 0.0)
        outs.append(jnp.einsum("bhqk,bhkd->bhqd", w, vb))
    return jnp.concatenate(outs, axis=2).astype(v.dtype)


def gla_chunked(q, k, v, log_a):
    B, S, H, Dk = q.shape
    Dv = v.shape[-1]
    N = S // CHUNK
    f32 = jnp.float32
    q = (q.astype(f32) * Dk ** -0.5).reshape(B, N, CHUNK, H, Dk)
    k = k.astype(f32).reshape(B, N, CHUNK, H, Dk)
    v = v.astype(f32).reshape(B, N, CHUNK, H, Dv)
    g = log_a.astype(f32).reshape(B, N, CHUNK, H, Dk)
    b = jnp.cumsum(g, axis=2)
    b_ref = b[:, :, CHUNK // 2 - 1:CHUNK // 2]
    q_in = q * jnp.exp(b - b_ref)
    k_in = k * jnp.exp(b_ref - b)
    scores = jnp.einsum("bnthd,bnshd->bnhts", q_in, k_in)
    causal = jnp.tril(jnp.ones((CHUNK, CHUNK), dtype=bool))
    scores = jnp.where(causal, scores, 0.0)
    o_intra = jnp.einsum("bnhts,bnshv->bnthv", scores, v)
    b_last = b[:, :, -1]
    k_dec = k * jnp.exp(b_last[:, :, None] - b)
    chunk_upd = jnp.einsum("bnshk,bnshv->bnhkv", k_dec, v)
    decay = jnp.exp(b_last)

    def step(state, inp):
        dec, upd = inp
        return dec[..., None] * state + upd, state

    init = jnp.zeros((B, H, Dk, Dv), f32)
    _, prev = lax.scan(step, init, (jnp.moveaxis(decay, 1, 0), jnp.moveaxis(chunk_upd, 1, 0)))
    prev = jnp.moveaxis(prev, 0, 1)
    o_inter = jnp.einsum("bnthk,bnhkv->bnthv", q * jnp.exp(b), prev)
    return (o_intra + o_inter).reshape(B, S, H, Dv)


def causal_depthwise_conv(u, w, bias):
    C = u.shape[-1]
    y = lax.conv_general_dilated(
        u, w[:, None, :].astype(u.dtype), window_strides=(1,),
        padding=[(CONV_WIDTH - 1, 0)], dimension_numbers=("NWC", "WIO", "NWC"),
        feature_group_count=C)
    return y + bias


def hybrid_layer(x, w_in, gate_up, gate_bias, gla_norm_g, w_out, ln1_g, ln1_b,
                 w_up, conv_w, conv_b, w_down, ln2_g, ln2_b):
    B, S, _ = x.shape
    proj = x @ w_in
    sb_q, sb_k, sb_v, gq, gk, gv, gg, ga = jnp.split(proj, IN_SPLITS, axis=-1)

    def to_heads(t):
        return t.reshape(B, S, SB_HEADS, SB_HEAD_DIM).transpose(0, 2, 1, 3)

    sb_o = stick_breaking_attention(to_heads(sb_q), to_heads(sb_k), to_heads(sb_v))
    sb_o = sb_o.transpose(0, 2, 1, 3).reshape(B, S, SB_WIDTH)

    log_a = jax.nn.log_sigmoid((ga @ gate_up + gate_bias).astype(jnp.float32)) / GLA_TAU
    o = gla_chunked(gq.reshape(B, S, GLA_HEADS, GLA_KEY_DIM),
                    gk.reshape(B, S, GLA_HEADS, GLA_KEY_DIM),
                    gv.reshape(B, S, GLA_HEADS, GLA_VAL_DIM),
                    log_a.reshape(B, S, GLA_HEADS, GLA_KEY_DIM))
    o = o * lax.rsqrt(jnp.mean(jnp.square(o), axis=-1, keepdims=True) + RMS_EPS)
    o = o * gla_norm_g.astype(jnp.float32)
    gla_o = (o.reshape(B, S, GLA_WIDTH) * jax.nn.silu(gg.astype(jnp.float32))).astype(x.dtype)

    mix = jnp.concatenate([sb_o, gla_o], axis=-1) @ w_out
    h = layer_norm(DN_ALPHA * x + mix, ln1_g, ln1_b)

    u = causal_depthwise_conv(h @ w_up, conv_w, conv_b)
    a, c = jnp.split(u, 2, axis=-1)
    f = (jax.nn.gelu(a, approximate=False) * c) @ w_down
    return layer_norm(DN_ALPHA * h + f, ln2_g, ln2_b)


def setup_inputs(seed: int = 0) -> dict:
    key = jax.random.key(seed)
    ks = jax.random.split(key, 24)
    d = D_MODEL
    nrm = lambda k, shape, s: jax.random.normal(k, shape, jnp.float32) * s
    in_scale = d ** -0.5
    pieces = [
        nrm(ks[1], (DEPTH, d, SB_WIDTH), in_scale),
        nrm(ks[2], (DEPTH, d, SB_WIDTH), in_scale),
        nrm(ks[3], (DEPTH, d, SB_WIDTH), in_scale * DN_BETA),
        nrm(ks[4], (DEPTH, d, GLA_HEADS * GLA_KEY_DIM), in_scale),
        nrm(ks[5], (DEPTH, d, GLA_HEADS * GLA_KEY_DIM), in_scale),
        nrm(ks[6], (DEPTH, d, GLA_WIDTH), in_scale * DN_BETA),
        nrm(ks[7], (DEPTH, d, GLA_WIDTH), in_scale),
        nrm(ks[8], (DEPTH, d, GLA_GATE_RANK), in_scale),
    ]
    return {
        "x": nrm(ks[0], (BATCH, SEQ, d), 1.0),
        "w_in": jnp.concatenate(pieces, axis=-1),
        "gate_up": nrm(ks[9], (DEPTH, GLA_GATE_RANK, GLA_HEADS * GLA_KEY_DIM), GLA_GATE_RANK ** -0.5),
        "gate_bias": nrm(ks[10], (DEPTH, GLA_HEADS * GLA_KEY_DIM), 0.1),
        "gla_norm_g": 1.0 + nrm(ks[11], (DEPTH, GLA_VAL_DIM), 0.02),
        "w_out": nrm(ks[12], (DEPTH, MIX_WIDTH, d), MIX_WIDTH ** -0.5 * DN_BETA),
        "ln1_g": 1.0 + nrm(ks[13], (DEPTH, d), 0.02),
        "ln1_b": nrm(ks[14], (DEPTH, d), 0.02),
        "w_up": nrm(ks[15], (DEPTH, d, 2 * D_FF), d ** -0.5 * DN_BETA),
        "conv_w": nrm(ks[16], (DEPTH, CONV_WIDTH, 2 * D_FF), CONV_WIDTH ** -0.5),
        "conv_b": nrm(ks[17], (DEPTH, 2 * D_FF), 0.02),
        "w_down": nrm(ks[18], (DEPTH, D_FF, d), D_FF ** -0.5 * DN_BETA),
        "ln2_g": 1.0 + nrm(ks[19], (DEPTH, d), 0.02),
        "ln2_b": nrm(ks[20], (DEPTH, d), 0.02),
    }


def reference(x, w_in, gate_up, gate_bias, gla_norm_g, w_out, ln1_g, ln1_b,
              w_up, conv_w, conv_b, w_down, ln2_g, ln2_b):
    for l in range(DEPTH):
        x = hybrid_layer(x, w_in[l], gate_up[l], gate_bias[l], gla_norm_g[l], w_out[l],
                         ln1_g[l], ln1_b[l], w_up[l], conv_w[l], conv_b[l], w_down[l],
                         ln2_g[l], ln2_b[l])
    return x
```

```python
import contextlib
import numpy as np
import concourse.bass as bass
import concourse.mybir as mybir
from concourse.bass_utils import run_bass_kernel_spmd

F32 = mybir.dt.float32
BF16 = mybir.dt.bfloat16
AF = mybir.ActivationFunctionType
ALU = mybir.AluOpType

D = 1024
S = 4096
NCORE = 8
DFF = 2816
NPAIR = DFF // 128
ALPHA = float((2.0 * 1) ** 0.25)
LN_EPS = 1e-5
RMS_EPS = 1e-6
LN8 = float(np.log(0.125))
W1C = 1808
C_Q, C_K, C_GQ, C_GK, C_GG, C_GA, C_V, C_GV = 0, 256, 512, 640, 768, 1024, 1040, 1552
K_ID, K_NTRI, K_NONE, K_MASKL, K_M1, K_M3, K_IND, K_M2, K_ONES, K_ZERO, K_MEAN, K_MASKL2, K_END = (
    0, 128, 256, 384, 512, 640, 768, 776, 904, 1032, 1160, 1288, 1544)
GRP = 384
GROUPS = [(0, 2), (2, 3), (5, 3), (8, 3), (11, 3), (14, 2)]


class _Rec:
    def __init__(self):
        self.call = None

    def __getattr__(self, name):
        def f(*a, **k):
            self.call = (name, a, k)
        return f


def _record(fn):
    r = _Rec()
    fn(r)
    assert r.call is not None
    return r.call


class Prog:
    ENGS = ("pe", "act", "dve", "pool", "sp")

    def __init__(self):
        self.ops = {e: [] for e in self.ENGS}
        self.count = {e: 0 for e in self.ENGS}
        self.known = {e: {} for e in self.ENGS}
        self.last_w = {}
        self.readers = {}
        self.dma_cnt = {}

    def _waits(self, eng, reads, writes):
        waits = {}

        def need(dep, war=False):
            key, val, deng = dep
            if deng == eng and eng == "pe":
                return
            if waits.get(key, 0) < val:
                waits[key] = val
        for r in reads:
            if r in self.last_w:
                need(self.last_w[r])
            if len(r) == 2 and r[0] == "b" and r[1].isdigit():
                for rd in self.readers.get(r, ()):
                    if rd[2] != eng:
                        need(rd)
        for w in writes:
            if w in self.last_w:
                need(self.last_w[w])
            for rd in self.readers.get(w, ()):
                need(rd, war=True)
        fin = []
        for k, v in waits.items():
            if self.known[eng].get(k, 0) < v:
                self.known[eng][k] = v
                fin.append((k, v))
        return fin

    def _commit(self, me, reads, writes):
        for r in reads:
            self.readers.setdefault(r, []).append(me)
        for w in writes:
            self.last_w[w] = me
            self.readers[w] = []

    def op(self, eng, fn, reads=(), writes=()):
        fin = self._waits(eng, reads, writes)
        self.count[eng] += 1
        me = (eng, self.count[eng], eng)
        self.ops[eng].append((fin, _record(fn), eng, 1))
        self._commit(me, reads, writes)

    def dma(self, queue, fn, sem, reads=(), writes=(), inc=16):
        fin = self._waits(queue, reads, writes)
        self.dma_cnt[sem] = self.dma_cnt.get(sem, 0) + inc
        me = (sem, self.dma_cnt[sem], "dma")
        self.ops[queue].append((fin, _record(fn), sem, inc))
        self._commit(me, reads, writes)

    def final_wait(self, eng, res):
        fin = self._waits(eng, res, ())
        self.ops[eng].append((fin, None, None, 0))


def build():
    import os
    STOP = os.environ.get('K_STOP', 'full')
    RUN2 = STOP not in ('p1',)
    K_NG = int(os.environ.get('K_NG', '8'))
    RG = [[0, 1]] if os.environ.get('K_RG') == '2' else [[0, 1], [2, 3], [4, 5], [6, 7]]
    K_PART = int(os.environ.get('K_PART', '9'))
    K_SKIP = os.environ.get('K_SKIP', '')
    RUNCC = STOP not in ('p1', 'p2')
    RUN3 = STOP not in ('p1', 'p2', 'cc')
    nc = bass.Bass("TRN2", target_bir_lowering=False)
    P = Prog()
    global _LASTP
    _LASTP = P
    es = contextlib.ExitStack()

    def dram(name, shape, dt=F32, kind="ExternalInput"):
        return nc.dram_tensor(name, list(shape), dt, kind=kind).ap()

    xT = dram("xT", [8, 128, 8, 512])
    xo = dram("xo", [17 * 128, D])
    w1 = dram("w1", [128, 8, W1C])
    gup = dram("gup", [16, 128])
    gbias = dram("gbias", [1, 128])
    gng = dram("gng", [128, 1])
    wout = dram("wout", [128, 8, D])
    lnp = dram("lnp", [4, D])
    wup = dram("wup", [NPAIR, 128, 2, 8, 128])
    convp = dram("convp", [128, 44, 4])
    wdown = dram("wdown", [128, NPAIR, D])
    flag = dram("flag", [128, 2])
    cst = dram("cst", [128, K_END])
    out = dram("out", [2048, D], kind="ExternalOutput")
    wup_b = dram("wup_b", [NPAIR, 128, 2, 8, 128], BF16, kind="Internal")
    wout_b = dram("wout_b", [128, 8, D], BF16, kind="Internal")
    wdown_b = dram("wdown_b", [128, NPAIR, D], BF16, kind="Internal")
    mix_src = [dram(f"mix_src{c}", [128, S], BF16, kind="Internal") for c in range(4)]
    mix_ag = [dram(f"mix_ag{c}", [256, S], BF16, kind="Internal") for c in range(4)]
    TSPL = 3456
    mix_src1b = dram("mix_src1b", [128, S - TSPL], BF16, kind="Internal")
    mix_ag1b = dram("mix_ag1b", [256, S - TSPL], BF16, kind="Internal")
    mix_src1a = dram("mix_src1a", [128, TSPL], BF16, kind="Internal")
    mix_ag1a = dram("mix_ag1a", [256, TSPL], BF16, kind="Internal")

    def sb(name, shape, dt, stack=es):
        return stack.enter_context(nc.sbuf_tensor(name, list(shape), dt))

    sems = {}

    def sem(name):
        if name not in sems:
            sems[name] = es.enter_context(nc.semaphore(name))
        return name
    for e in ("pe", "act", "dve", "pool"):
        sem(e)

    bankp = [es.enter_context(nc.psum_tensor(f"bankp{i}", [128, 2, 512], F32)) for i in range(4)]
    banks = [bankp[i // 2][:, i % 2, :] for i in range(8)]
    B = [f"b{i}" for i in range(8)]

    cb = sb("cb", [128, K_END], BF16)
    identF = sb("identF", [128, 128], F32)
    flg = sb("flg", [128, 2], F32)
    gngt = sb("gngt", [128, 1], F32)
    P.dma("pool", lambda g: g.dma_start(out=cb[:], in_=cst), sem("c_cb"), writes=["cb"])
    P.dma("sp", lambda q: q.dma_start(out=identF[:], in_=cst[:, K_ID:K_ID + 128]), sem("c_id"), writes=["identF"])
    P.dma("sp", lambda q: q.dma_start(out=flg[:], in_=flag), sem("c_fl"), writes=["flg"])
    P.dma("sp", lambda q: q.dma_start(out=gngt[:], in_=gng), sem("c_gn"), writes=["gng"])
    ident_b = cb[:, K_ID:K_ID + 128]
    negtri = cb[:, K_NTRI:K_NTRI + 128]
    negones = cb[:, K_NONE:K_NONE + 128]
    maskL = cb[:, K_MASKL:K_MASKL + 128]
    M1 = cb[:, K_M1:K_M1 + 128]
    M3 = cb[:, K_M3:K_M3 + 128]
    IND = cb[:, K_IND:K_IND + 2]
    M2 = cb[:, K_M2:K_M2 + 128]
    ones_b = cb[:, K_ONES:K_ONES + 128]
    zeros_b = cb[:, K_ZERO:K_ZERO + 128]
    mean_b = cb[:, K_MEAN:K_MEAN + 128]
    maskL2 = cb[:, K_MASKL2:K_MASKL2 + 256].rearrange("p (a b) -> p a b", a=2)

    sA = contextlib.ExitStack()
    QT = sb("QT", [128, 2, S], BF16, sA)
    KT = sb("KT", [128, 2, S], BF16, sA)
    Vp = sb("Vp", [128, 32, 4, 128], BF16, sA)
    mixT = sb("mixT", [128, 4, S], BF16, sA)

    def exchange(c, part=None):
        if not RUNCC:
            return
        if part is None:
            src, dst, tsl_, tg_list, tag = mix_src[c], mix_ag[c], slice(0, S), range(8), f"{c}"
        elif part == "a":
            src, dst, tsl_, tg_list, tag = mix_src1a, mix_ag1a, slice(0, TSPL), range(7), "1a"
        else:
            src, dst, tsl_, tg_list, tag = mix_src1b, mix_ag1b, slice(TSPL, S), range(6, 8), "1b"
        P.dma("sp", lambda q: q.dma_start(out=src, in_=mixT[:, c, tsl_]), sem(f"mxs{tag}"),
              reads=[f"mixT{c}_{t}" for t in tg_list], writes=[f"mix_src{tag}"])
        P.dma("pool", lambda g: g.collective_compute("AllGather", ALU.bypass, replica_groups=RG, ins=[src], outs=[dst]), sem(f"cc{tag}"),
              reads=[f"mix_src{tag}"], writes=[f"mix_ag{tag}"], inc=1)

    s1 = contextlib.ExitStack()
    w1b = sb("w1b", [128, 8, W1C], BF16, s1)
    xTb = [sb(f"xTb{i}", [128, 8, 512], BF16, s1) for i in range(2)]
    gupb = sb("gupb", [16, 128], BF16, s1)
    gbb = sb("gbb", [1, 128], BF16, s1)
    gq_sb = [sb(f"gq_sb{i}", [128, 512], F32, s1) for i in range(2)]
    gk_sb = [sb(f"gk_sb{i}", [128, 512], F32, s1) for i in range(2)]
    sg = [[sb(f"sg{i}_{h}", [128, 512], F32, s1) for h in range(2)] for i in range(2)]
    gaT = [sb(f"gaT{i}", [16, 512], BF16, s1) for i in range(2)]
    gv_sb = [sb(f"gv_sb{i}", [128, 4, 256], BF16, s1) for i in range(2)]
    gk_tok = [sb(f"gk_tok{i}", [128, 4, 128], F32, s1) for i in range(2)]
    l_tok = sb("l_tok", [128, 4, 128], BF16, s1)
    E1m = sb("E1m", [128, 512], F32, s1)
    E1p = sb("E1p", [128, 512], F32, s1)
    E3 = sb("E3", [128, 512], F32, s1)
    E2 = sb("E2", [128, 512], F32, s1)
    e_g, rstd_g, t1_g = E3, E1m, E1p
    dec = sb("dec", [128, 8], F32, s1)
    q_in = sb("q_in", [128, 512], BF16, s1)
    k_in = sb("k_in", [128, 512], BF16, s1)
    qb = sb("qb", [128, 512], F32, s1)
    k_dec = sb("k_dec", [128, 4, 128], BF16, s1)
    sc_sb = [sb(f"sc_sb{i}", [128, 128], BF16, s1) for i in range(2)]
    St = sb("St", [128, 128], F32, s1)
    Sb = [sb(f"Sb{i}", [128, 128], BF16, s1) for i in range(2)]
    osq2 = [sb(f"osq{h}", [128, 512], BF16, s1) for h in range(2)]

    def load_x(tg):
        s = tg % 2
        P.dma("pool", lambda g: g.dma_start(out=xTb[s][:], in_=xT[tg]), sem(f"x{s}"), writes=[f"xTb{s}"])

    load_x(0)
    WBLK = [(0, 512), (512, 1040), (1040, W1C)]
    for bi, (c0_, c1_) in enumerate(WBLK):
        P.dma("pool", lambda g: g.dma_start(out=w1b[:, :, c0_:c1_], in_=w1[:, :, c0_:c1_]), sem(f"w1_{bi}"), writes=[f"w1b{bi}"])
    P.dma("pool", lambda g: g.dma_start(out=gupb[:], in_=gup), sem("c_gu"), writes=["gupb"])
    P.dma("pool", lambda g: g.dma_start(out=gbb[:], in_=gbias), sem("c_gb"), writes=["gbb"])
    P.op("dve", lambda v: v.memset(St[:], 0.0), writes=["St"])
    P.op("dve", lambda v: v.memset(Sb[0][:], 0.0), writes=["Sb0"])

    def wres(c0):
        return ["w1b0" if c0 < 512 else ("w1b1" if c0 < 1040 else "w1b2")]

    FB = [3, 4]

    def proj_units(tg):
        s = tg % 2
        xt = xTb[s]
        XR = [f"xTb{s}"]
        t0 = tg * 512
        tsl = slice(t0, t0 + 512)
        gb = tg % 2
        units = []
        fcnt = [0]

        def fm(c0, m, evac):
            def u():
                bank = FB[fcnt[0] % 2]
                fcnt[0] += 1
                for k in range(8):
                    P.op("pe", lambda pe: pe.matmul(banks[bank][0:m, :], lhsT=w1b[:, k, c0:c0 + m], rhs=xt[:, k, :],
                                                    start=(k == 0), stop=(k == 7)), reads=wres(c0) + XR, writes=[B[bank]])
                evac(bank)
            units.append(u)
        for hp in range(2):
            fm(C_Q + 128 * hp, 128, lambda bank, hp=hp: P.op(
                "act", lambda a: a.activation(out=QT[:, hp, tsl], in_=banks[bank][:, :], func=AF.Copy, scale=0.125),
                reads=[B[bank]], writes=[f"QT{hp}_{tg}"]))
        for hp in range(2):
            fm(C_K + 128 * hp, 128, lambda bank, hp=hp: P.op(
                "dve", lambda v: v.tensor_copy(out=KT[:, hp, tsl], in_=banks[bank][:, :]), reads=[B[bank]], writes=[f"KT{hp}_{tg}"]))
        fm(C_GQ, 128, lambda bank: P.op("dve", lambda v: v.tensor_copy(out=gq_sb[gb][:], in_=banks[bank][:, :]),
                                        reads=[B[bank]], writes=[f"gq_sb{gb}"]))
        fm(C_GK, 128, lambda bank: P.op("dve", lambda v: v.tensor_copy(out=gk_sb[gb][:], in_=banks[bank][:, :]),
                                        reads=[B[bank]], writes=[f"gk_sb{gb}"]))
        for h in range(2):
            fm(C_GG + 128 * h, 128, lambda bank, h=h: P.op(
                "act", lambda a: a.activation(out=sg[gb][h][:], in_=banks[bank][:, :], func=AF.Silu), reads=[B[bank]], writes=[f"sg{gb}_{h}"]))
        fm(C_GA, 16, lambda bank: P.op("dve", lambda v: v.tensor_copy(out=gaT[gb][:], in_=banks[bank][0:16, :]),
                                       reads=[B[bank]], writes=[f"gaT{gb}"]))
        for i in range(4):
            blk = tg * 4 + i
            lt = xt[:, :, 128 * i:128 * i + 128]

            def uv(i=i, blk=blk, lt=lt):
                for k in range(8):
                    P.op("pe", lambda pe: pe.matmul(banks[7][:, :], lhsT=lt[:, k, :], rhs=w1b[:, k, C_V:C_V + 512],
                                                    start=(k == 0), stop=(k == 7)), reads=wres(C_V) + XR, writes=[B[7]])
                P.op("dve", lambda v: v.tensor_copy(out=Vp[:, blk, :, :].rearrange("p a b -> p (a b)"), in_=banks[7][:, :]),
                     reads=[B[7]], writes=[f"Vp_{blk}"])
            units.append(uv)

            def ug(i=i, lt=lt):
                bank = FB[fcnt[0] % 2]
                fcnt[0] += 1
                for k in range(8):
                    P.op("pe", lambda pe: pe.matmul(banks[bank][:, 0:256], lhsT=lt[:, k, :], rhs=w1b[:, k, C_GV:C_GV + 256],
                                                    start=(k == 0), stop=(k == 7)), reads=wres(C_GV) + XR, writes=[B[bank]])
                for k in range(8):
                    P.op("pe", lambda pe: pe.matmul(banks[bank][:, 256:384], lhsT=lt[:, k, :], rhs=w1b[:, k, C_GK:C_GK + 128],
                                                    start=(k == 0), stop=(k == 7)), reads=wres(C_GK) + XR, writes=[B[bank]])
                P.op("act", lambda a: a.activation(out=gv_sb[gb][:, i, :], in_=banks[bank][:, 0:256], func=AF.Copy),
                     reads=[B[bank]], writes=[f"gv_sb{gb}"])
                P.op("act", lambda a: a.activation(out=gk_tok[gb][:, i, :], in_=banks[bank][:, 256:384], func=AF.Copy),
                     reads=[B[bank]], writes=[f"gk_tok{gb}"])
            units.append(ug)
        return units

    sb_idx = [0]

    def gla_stages(tg):
        gb = tg % 2
        t0 = tg * 512
        tsl = slice(t0, t0 + 512)
        GQ, GK, GV, GKT, GA = f"gq_sb{gb}", f"gk_sb{gb}", f"gv_sb{gb}", f"gk_tok{gb}", f"gaT{gb}"
        st = []

        def s_gate():
            for i in range(4):
                P.op("pe", lambda pe: pe.matmul(banks[5][:, 128 * i:128 * i + 128], lhsT=gaT[gb][:, 128 * i:128 * i + 128],
                                                rhs=gupb[:, :], start=True, stop=False), reads=[GA, "gupb"], writes=[B[5]])
                P.op("pe", lambda pe: pe.matmul(banks[5][:, 128 * i:128 * i + 128], lhsT=ones_b[0:1, :], rhs=gbb[:, :],
                                                start=False, stop=True), reads=["cb", "gbb"], writes=[B[5]])
            P.op("act", lambda a: a.activation(out=e_g[:], in_=banks[5][:, :], func=AF.Exp, scale=-1.0), reads=[B[5]], writes=["E3"])
            P.op("act", lambda a: a.activation(out=l_tok[:, :, :].rearrange("p a b -> p (a b)"), in_=e_g[:], func=AF.Ln, bias=1.0),
                 reads=["E3"], writes=["l_tok"])
        st.append(s_gate)

        def s_cum1():
            for i in range(4):
                P.op("pe", lambda pe: pe.matmul(banks[2][:, 128 * i:128 * i + 128], lhsT=l_tok[:, i, :], rhs=M1, start=True, stop=True),
                     reads=["l_tok", "cb"], writes=[B[2]])
                P.op("pe", lambda pe: pe.matmul(banks[6][:, 256 + 2 * i:258 + 2 * i], lhsT=l_tok[:, i, :], rhs=IND, start=True, stop=True),
                     reads=["l_tok", "cb"], writes=[B[6]])
            P.op("act", lambda a: a.activation(out=E1m[:], in_=banks[2][:, :], func=AF.Exp, scale=-1.0 / 16, bias=LN8), reads=[B[2]], writes=["E1m"])
            P.op("act", lambda a: a.activation(out=E1p[:], in_=banks[2][:, :], func=AF.Exp, scale=1.0 / 16), reads=[B[2]], writes=["E1p"])
            P.op("act", lambda a: a.activation(out=dec[:], in_=banks[6][:, 256:264], func=AF.Exp, scale=-1.0 / 16), reads=[B[6]], writes=["dec"])
            P.op("pool", lambda g: g.tensor_tensor(out=q_in[:], in0=gq_sb[gb][:], in1=E1m[:], op=ALU.mult), reads=[GQ, "E1m"], writes=["q_in"])
            P.op("dve", lambda v: v.tensor_tensor(out=k_in[:], in0=gk_sb[gb][:], in1=E1p[:], op=ALU.mult), reads=[GK, "E1p"], writes=["k_in"])
        st.append(s_cum1)

        def s_cum2():
            for i in range(4):
                P.op("pe", lambda pe: pe.matmul(banks[2][:, 128 * i:128 * i + 128], lhsT=l_tok[:, i, :], rhs=M3, start=True, stop=True),
                     reads=["l_tok", "cb"], writes=[B[2]])
            P.op("act", lambda a: a.activation(out=E3[:], in_=banks[2][:, :], func=AF.Exp, scale=-1.0 / 16, bias=LN8), reads=[B[2]], writes=["E3"])
            P.op("pool", lambda g: g.tensor_tensor(out=qb[:], in0=gq_sb[gb][:], in1=E3[:], op=ALU.mult), reads=[GQ, "E3"], writes=["qb"])
        st.append(s_cum2)

        def s_cum3():
            for i in range(4):
                P.op("pe", lambda pe: pe.matmul(banks[2][:, 128 * i:128 * i + 128], lhsT=M2, rhs=l_tok[:, i, :], start=True, stop=True),
                     reads=["l_tok", "cb"], writes=[B[2]])
            P.op("act", lambda a: a.activation(out=E2[:], in_=banks[2][:, :], func=AF.Exp, scale=-1.0 / 16), reads=[B[2]], writes=["E2"])
            P.op("dve", lambda v: v.tensor_tensor(out=k_dec[:, :, :].rearrange("p a b -> p (a b)"),
                                                  in0=gk_tok[gb][:, :, :].rearrange("p a b -> p (a b)"), in1=E2[:], op=ALU.mult),
                 reads=[GKT, "E2"], writes=["k_dec"])
        st.append(s_cum3)
        OB = [0, 1]
        for i in range(4):
            cs = slice(128 * i, 128 * i + 128)

            def s_scores(i=i, cs=cs):
                for h in range(2):
                    hs = slice(64 * h, 64 * h + 64)
                    si = h
                    sbk = [5, 2][h]
                    P.op("pe", lambda pe: pe.matmul(banks[sbk][:, 0:128], lhsT=k_in[hs, cs], rhs=q_in[hs, cs], start=True, stop=True),
                         reads=["k_in", "q_in"], writes=[B[sbk]])
                for h in range(2):
                    si = h
                    sbk = [5, 2][h]
                    P.op("dve", lambda v: v.tensor_tensor(out=sc_sb[si][:], in0=banks[sbk][:, 0:128], in1=M3, op=ALU.mult),
                         reads=[B[sbk], "cb"], writes=[f"sc{si}"])
            st.append(s_scores)

            def s_intra(i=i, cs=cs):
                for h in range(2):
                    P.op("pe", lambda pe: pe.matmul(banks[OB[h]][:, cs], lhsT=gv_sb[gb][:, i, 128 * h:128 * h + 128], rhs=sc_sb[h][:],
                                                    start=True, stop=False), reads=[GV, f"sc{h}"], writes=[B[OB[h]]])
            st.append(s_intra)
            for c in range(2):
                def s_chunk(i=i, c=c):
                    ci = 2 * i + c
                    ccs = slice(128 * i + 64 * c, 128 * i + 64 * c + 64)
                    rows = slice(64 * c, 64 * c + 64)
                    for h in range(2):
                        hs = slice(64 * h, 64 * h + 64)
                        P.op("pe", lambda pe: pe.matmul(banks[OB[h]][:, ccs], lhsT=St[hs, :], rhs=qb[hs, ccs], start=False, stop=(c == 1)),
                             reads=["St", "qb"], writes=[B[OB[h]]])
                    P.op("pe", lambda pe: pe.matmul(banks[6][:, 0:256], lhsT=k_dec[rows, i, :], rhs=gv_sb[gb][rows, i, :], start=True, stop=True),
                         reads=["k_dec", GV], writes=[B[6]])
                    for h in range(2):
                        hs = slice(64 * h, 64 * h + 64)
                        P.op("dve", lambda v: v.scalar_tensor_tensor(out=St[hs, :], in0=St[hs, :], scalar=dec[hs, ci:ci + 1],
                                                                     in1=banks[6][hs, 128 * h:128 * h + 128], op0=ALU.mult, op1=ALU.add),
                             reads=["St", "dec", B[6]], writes=["St"])
                st.append(s_chunk)
        def s_out():
            rbuf, rtok = [E1m, E3], ["E1m", "E3"]
            tbuf, ttok = [E1p, E2], ["E1p", "E2"]
            mbank = [5, 2]
            for h in range(2):
                P.op("act", lambda a: a.activation(out=osq2[h][:], in_=banks[OB[h]][:, :], func=AF.Square), reads=[B[OB[h]]], writes=[f"osq{h}"])
                P.op("pe", lambda pe: pe.matmul(banks[mbank[h]][:, :], lhsT=mean_b, rhs=osq2[h][:], start=True, stop=True),
                     reads=[f"osq{h}", "cb"], writes=[B[mbank[h]]])
            for h in range(2):
                P.op("act", lambda a: a.activation(out=rbuf[h][:], in_=banks[mbank[h]][:, :], func=AF.Ln, bias=RMS_EPS),
                     reads=[B[mbank[h]]], writes=[rtok[h]])
                P.op("act", lambda a: a.activation(out=rbuf[h][:], in_=rbuf[h][:], func=AF.Exp, scale=-0.5), reads=[rtok[h]], writes=[rtok[h]])
            for h in range(2):
                P.op("dve", lambda v: v.scalar_tensor_tensor(out=tbuf[h][:], in0=banks[OB[h]][:, :], scalar=gngt[:, 0:1], in1=rbuf[h][:],
                                                             op0=ALU.mult, op1=ALU.mult), reads=[B[OB[h]], "gng", rtok[h]], writes=[ttok[h]])
            for h in range(2):
                P.op("pool", lambda g: g.tensor_tensor(out=mixT[:, 2 + h, tsl], in0=tbuf[h][:], in1=sg[gb][h][:], op=ALU.mult),
                     reads=[ttok[h], f"sg{gb}_{h}"], writes=[f"mixT{2 + h}_{tg}"])
        st.append(s_out)
        return st

    for u in proj_units(0)[:int(os.environ.get('K_PU', '99'))]:
        u()
    bg = []
    if RUN3:
        for k in range(4):
            bg.append(lambda k=k: P.dma("pool", lambda g: g.dma_start(out=wout_b[:, 2 * k:2 * k + 2, :], in_=wout[:, 2 * k:2 * k + 2, :]),
                                        sem("woc"), writes=["wout_b"]))
        for k in range(11):
            bg.append(lambda k=k: P.dma("pool", lambda g: g.dma_start(out=wdown_b[:, 2 * k:2 * k + 2, :], in_=wdown[:, 2 * k:2 * k + 2, :]),
                                        sem("wdc"), writes=["wdown_b"]))
    if RUN2:
        for j in range(NPAIR):
            bg.append(lambda j=j: P.dma("pool", lambda g: g.dma_start(out=wup_b[j], in_=wup[j]), sem("wupc"), writes=["wup_b"]))
    load_x(1)
    for tg in range(K_NG):
        if tg + 2 < 8:
            load_x(tg + 2)
        pu = proj_units(tg + 1) if tg + 1 < K_NG else []
        gs = gla_stages(tg)[:int(os.environ.get('K_GS', '99'))]
        npu, ngs = len(pu), len(gs)
        done = 0
        skip = {5, 9, 13, 17}
        for k in range(ngs):
            gs[k]()
            if k not in skip and done < npu:
                pu[done]()
                done += 1
        while done < npu:
            pu[done]()
            done += 1
    exchange(2)
    exchange(3)
    P.op("pool", lambda g: g.memset(q_in[:, 0:1], 0.0),
         reads=["gq_sb0", "gq_sb1", "gk_sb0", "gk_sb1", "sg0_0", "sg0_1", "sg1_0", "sg1_1", "gaT0", "gaT1", "gv_sb0", "gv_sb1",
                "gk_tok0", "gk_tok1", "l_tok", "E1m", "E1p", "E3", "E2", "dec", "q_in", "k_in", "qb", "k_dec", "sc0", "sc1", "St",
                "Sb0", "Sb1", "osq0", "osq1", "w1b0", "w1b1", "w1b2", "xTb0", "xTb1", "gupb", "gbb"], writes=["P1DONE"])
    s1.close()
    sW = contextlib.ExitStack()
    woutb = sW.enter_context(nc.sbuf_tensor("woutb", [128, 8, D], BF16, side="right"))
    wdownb = sW.enter_context(nc.sbuf_tensor("wdownb", [128, NPAIR, D], BF16, side="right"))
    lnt = [sW.enter_context(nc.sbuf_tensor(f"lnt{i}", [128, D], F32, side="right")) for i in range(4)]
    cvp = sW.enter_context(nc.sbuf_tensor("cvp", [128, 44, 4], F32, side="right"))
    s2 = contextlib.ExitStack()
    e_buf = [sb(f"e_buf{i}", [128, 2, 512], F32, s2) for i in range(2)]
    sp_buf = [sb(f"sp_buf{i}", [128, 2, 512], BF16, s2) for i in range(3)]
    w_buf = [sb(f"w_buf{i}", [128, 2, 512], BF16, s2) for i in range(3)]
    Rf = sb("Rf", [128, 2, 512], F32, s2)
    Rb = [sb(f"Rb{j}", [128, 2, 512], BF16, s2) for j in range(2)]
    P2RES = ["e0", "e1", "sp0", "sp1", "sp2", "wb0", "wb1", "wb2", "Rf", "Rb0", "Rb1"]
    P.op("pool", lambda g: g.memset(e_buf[0][:, 0, 0:1], 0.0), reads=["P1DONE"], writes=P2RES)
    wup_issued = 0
    NE, NSP, NW = 2, 3, 3
    tiles = []
    for hp in range(2 if RUN2 else 0):
        for G in range(8):
            ntile = 4 * G + 4
            for i in range(ntile - 1, -1, -1):
                tiles.append(dict(hp=hp, G=G, i=i, first=(i == ntile - 1), gend=(i == 0), idx=len(tiles)))
    rbi = 0
    for T in tiles:
        if T["first"]:
            rbi = 0
        T["rb_read"] = rbi % 2
        if T["i"] > 0:
            rbi += 1
            T["rb_write"] = rbi % 2
    ZA = bankp[0]
    ZB = [bankp[1], bankp[2]]
    ZAR = [B[0], B[1]]
    ZBR = [[B[2], B[3]], [B[4], B[5]]]

    def load_resident():
        for i in range(4):
            P.dma("sp", lambda q: q.dma_start(out=lnt[i][:], in_=lnp[i:i + 1, :].partition_broadcast(128)), sem("p3c"),
                  reads=["P1DONE"], writes=["lnt"])
        P.dma("sp", lambda q: q.dma_start(out=cvp[:], in_=convp), sem("p3cv"), reads=["P1DONE"], writes=["cvp"])
        P.dma("sp", lambda q: q.dma_start(out=woutb[:], in_=wout_b), sem("p3wo"), reads=["P1DONE", "wout_b"], writes=["woutb"])
        P.dma("sp", lambda q: q.dma_start(out=wdownb[:, 0:11, :], in_=wdown_b[:, 0:11, :]), sem("p3wd"), reads=["P1DONE", "wdown_b"], writes=["wdownb"])
        P.dma("sp", lambda q: q.dma_start(out=wdownb[:, 11:22, :], in_=wdown_b[:, 11:22, :]), sem("p3wd"), reads=["P1DONE", "wdown_b"], writes=["wdownb"])

    def geom(T):
        r = T["i"] - 4 * T["G"]
        c0 = max(0, 128 * r)
        return r, c0, slice(c0, 512), 512 * T["G"]

    def S1(T):
        nonlocal wup_issued
        hp, G, i, n = T["hp"], T["G"], T["i"], T["idx"]
        r, c0, cs, qsl0 = geom(T)
        ob = 6 + ((hp * 8 + G) % 2)
        if T["first"]:
            P.op("pe", lambda pe: pe.matmul(banks[ob][:, :], lhsT=zeros_b, rhs=QT[:, hp, qsl0:qsl0 + 512], start=True, stop=False),
                 reads=["cb", f"QT{hp}_{G}"], writes=[B[ob]])
            P.op("dve", lambda v: v.memset(Rf[:, :, :], 0.0), writes=["Rf"])
            for _ in range(2 if hp == 0 else 3):
                if bg:
                    bg.pop(0)()
            if hp == 1 and G == 0 and RUN3:
                load_resident()
        for hh in range(2):
            pb = slice(64 * hh, 64 * hh + 64)
            P.op("pe", lambda pe: pe.matmul(ZA[:, hh, cs], lhsT=KT[pb, hp, 128 * i:128 * i + 128], rhs=QT[pb, hp, qsl0 + c0:qsl0 + 512],
                                            start=True, stop=True), reads=[f"KT{hp}_{i // 4}", f"QT{hp}_{G}"], writes=[ZAR[hh]])

    def S2a(T):
        n = T["idx"]
        r, c0, cs, qsl0 = geom(T)
        eb = n % NE
        P.op("act", lambda a: a.activation(out=e_buf[eb][:, :, cs], in_=ZA[:, :, cs], func=AF.Exp), reads=ZAR, writes=[f"e{eb}"])

    def S2b(T):
        i, n = T["i"], T["idx"]
        r, c0, cs, qsl0 = geom(T)
        eb, spb = n % NE, n % NSP
        P.op("act", lambda a: a.activation(out=sp_buf[spb][:, :, cs], in_=e_buf[eb][:, :, cs], func=AF.Ln, bias=1.0),
             reads=[f"e{eb}"], writes=[f"sp{spb}"])
        if r >= 0:
            ds = slice(128 * r, 128 * r + 128)
            P.op("dve", lambda v: v.tensor_tensor(out=sp_buf[spb][:, :, ds], in0=sp_buf[spb][:, :, ds], in1=maskL2, op=ALU.mult),
                 reads=[f"sp{spb}", "cb"], writes=[f"sp{spb}"])
        if i > 0:
            P.op("dve", lambda v: v.tensor_tensor(out=Rf[:, :, cs], in0=Rf[:, :, cs], in1=sp_buf[spb][:, :, cs], op=ALU.add),
                 reads=["Rf", f"sp{spb}"], writes=["Rf"])
            rbn = Rb[T["rb_write"]]
            P.op("dve", lambda v: v.tensor_copy(out=rbn[:, :, :], in_=Rf[:, :, :]), reads=["Rf"], writes=[f"Rb{T['rb_write']}"])

    def S3a(T):
        hp, G, i, n = T["hp"], T["G"], T["i"], T["idx"]
        r, c0, cs, qsl0 = geom(T)
        zb, spb = n % 2, n % NSP
        first = T["first"]
        for hh in range(2):
            pb = slice(64 * hh, 64 * hh + 64)
            P.op("pe", lambda pe: pe.matmul(ZB[zb][:, hh, cs], lhsT=KT[pb, hp, 128 * i:128 * i + 128], rhs=QT[pb, hp, qsl0 + c0:qsl0 + 512],
                                            start=True, stop=False), reads=[f"KT{hp}_{i // 4}", f"QT{hp}_{G}"], writes=[ZBR[zb][hh]])
            P.op("pe", lambda pe: pe.matmul(ZB[zb][:, hh, cs], lhsT=negtri, rhs=sp_buf[spb][:, hh, cs], start=False, stop=first),
                 reads=[f"sp{spb}", "cb"], writes=[ZBR[zb][hh]])
            if not first:
                rb = Rb[T["rb_read"]]
                P.op("pe", lambda pe: pe.matmul(ZB[zb][:, hh, cs], lhsT=negones, rhs=rb[:, hh, cs], start=False, stop=True),
                     reads=[f"Rb{T['rb_read']}", "cb"], writes=[ZBR[zb][hh]])

    def S3b(T):
        n = T["idx"]
        r, c0, cs, qsl0 = geom(T)
        zb, wb = n % 2, n % NW
        P.op("act", lambda a: a.activation(out=w_buf[wb][:, :, cs], in_=ZB[zb][:, :, cs], func=AF.Exp), reads=ZBR[zb], writes=[f"wb{wb}"])
        if r >= 0:
            ds = slice(128 * r, 128 * r + 128)
            P.op("dve", lambda v: v.tensor_tensor(out=w_buf[wb][:, :, ds], in0=w_buf[wb][:, :, ds], in1=maskL2, op=ALU.mult),
                 reads=[f"wb{wb}", "cb"], writes=[f"wb{wb}"])

    def S4(T):
        hp, G, i, n = T["hp"], T["G"], T["i"], T["idx"]
        r, c0, cs, qsl0 = geom(T)
        ob = 6 + ((hp * 8 + G) % 2)
        wb = n % NW
        for hh in range(2):
            hl = 2 * hp + hh
            P.op("pe", lambda pe: pe.matmul(banks[ob][:, cs], lhsT=Vp[:, i, hl, :], rhs=w_buf[wb][:, hh, cs], start=False,
                                            stop=(T["gend"] and hh == 1)),
                 reads=[f"Vp_{i}", f"wb{wb}"], writes=[B[ob]])
        if T["gend"]:
            P.op("dve", lambda v: v.tensor_copy(out=mixT[:, hp, qsl0:qsl0 + 512], in_=banks[ob][:, :]),
                 reads=[B[ob]], writes=[f"mixT{hp}_{G}"])
            if hp == 0 and G == 7:
                exchange(0)
            if hp == 1 and G == 6:
                exchange(1, "a")

    NT = len(tiles)
    for s_ in range(NT + 3):
        if s_ < NT:
            S1(tiles[s_])
            S2a(tiles[s_])
        if 0 <= s_ - 1 < NT:
            S3a(tiles[s_ - 1])
        if 0 <= s_ - 2 < NT:
            S3b(tiles[s_ - 2])
        if s_ < NT:
            S2b(tiles[s_])
        if 0 <= s_ - 3 < NT:
            S4(tiles[s_ - 3])
    while bg:
        bg.pop(0)()
    exchange(1, "b")
    allmix = [f"mixT{c}_{t}" for c in range(4) for t in range(8)]
    P.op("pool", lambda g: g.memset(e_buf[0][:, 0, 0:1], 0.0), reads=P2RES + ([f"mix_src{c}" for c in ("0", "1a", "1b", "2", "3")] if RUNCC else []) + [f"Vp_{b_}" for b_ in range(32)] + allmix, writes=["P2DONE"])
    s2.close()
    sA.close()

    s3 = contextlib.ExitStack()
    ca = sb("ca", [128, 8, GRP], BF16, s3)
    cbb = sb("cbb", [128, 8, GRP], BF16, s3)
    xh = [sb(f"xh{i}", [128, 3, D], F32, s3) for i in range(2)]
    hT = [sb(f"hT{i}", [128, 8, GRP + 2], BF16, s3) for i in range(2)]
    wupt = [sb(f"wupt{i}", [128, 2, 8, 128], BF16, s3) for i in range(8)]
    acc_a = [sb(f"acc_a{i}", [128, GRP], F32, s3) for i in range(2)]
    acc_c = [sb(f"acc_c{i}", [128, GRP], F32, s3) for i in range(2)]
    gel = [sb(f"gel{i}", [128, GRP], F32, s3) for i in range(2)]
    gT = sb("gT", [128, NPAIR, GRP], BF16, s3)
    ost = [sb(f"ost{i}", [128, D], F32, s3) for i in range(2)]
    st6 = sb("st6", [128, 12], F32, s3)
    mv = sb("mv", [128, 2], F32, s3)
    rs = sb("rs", [128, 1], F32, s3)
    nmr = sb("nmr", [128, 1], F32, s3)
    P3RES = [ "ca", "cbb", "caL0", "caL1", "caL2", "caL3", "cbbL0", "cbbL1", "cbbL2", "cbbL3", "xh0_0", "xh0_1", "xh0_2", "xh1_0", "xh1_1", "xh1_2", "hT0", "hT1", "wupt0", "wupt1", "wupt2", "wupt3", "wupt4", "wupt5", "wupt6", "wupt7",
             "acca0", "acca1", "accc0", "accc1", "gel0", "gel1", "gT", "ost0", "ost1", "st6", "mv", "rs", "nmr"]
    P.op("pool", lambda g: g.memset(st6[:, 0:1], 0.0), reads=["P2DONE"], writes=P3RES + ["P3GO"])
    def stage_a(xrow0, cand_t0, nt, xb, hi, hcol0, halo_only=False):
        n = 128 * nt
        hbuf = hT[hi]
        xhb = xh[xb]
        XR = lambda t: f"xh{xb}_{t}"
        pieces = []

        def prep():
            a0 = max(cand_t0, 0)
            b0 = 2048 + cand_t0
            for c in range(4):
                for (dstt, t_, semn, tok) in ((ca, a0, "canda", "ca"), (cbb, b0, "candb", "cbb")):
                    if c != 1:
                        srcap, off, tag = mix_ag[c], t_, f"{c}"
                    elif t_ + n <= TSPL:
                        srcap, off, tag = mix_ag1a, t_, "1a"
                    else:
                        assert t_ >= TSPL
                        srcap, off, tag = mix_ag1b, t_ - TSPL, "1b"
                    mv_ = srcap.rearrange("(r p) t -> p r t", p=128)
                    P.dma("sp", lambda q: q.dma_start(out=dstt[:, 2 * c:2 * c + 2, 0:n], in_=mv_[:, :, off:off + n]), sem(semn),
                          reads=[f"mix_ag{tag}"], writes=[f"{tok}L{c}"])
            P.dma("pool", lambda g: g.dma_start(out=xhb[:, 0:nt, :], in_=xo[xrow0:xrow0 + n, :].rearrange("(a p) d -> p a d", p=128)),
                  sem(f"xin{xb}"), writes=[XR(t) for t in range(nt)])
            CAL = [f"caL{c}" for c in range(4)]
            CBL = [f"cbbL{c}" for c in range(4)]
            P.op("dve", lambda v: v.tensor_scalar(out=cbb[:, :, 0:n], in0=cbb[:, :, 0:n], scalar1=flg[:, 0:1], scalar2=None, op0=ALU.mult),
                 reads=CBL + ["flg"], writes=["cbb"] + CBL)
            P.op("dve", lambda v: v.scalar_tensor_tensor(out=ca[:, :, 0:n], in0=ca[:, :, 0:n], scalar=flg[:, 1:2], in1=cbb[:, :, 0:n],
                                                         op0=ALU.mult, op1=ALU.add), reads=CAL + ["cbb", "flg"], writes=["ca"] + CAL)
        pieces.append(prep)

        def mmln(t):
            ts_ = slice(128 * t, 128 * t + 128)
            for half in range(2):
                for k in range(8):
                    P.op("pe", lambda pe: pe.matmul(banks[4 + half][:, :], lhsT=ca[:, k, ts_], rhs=woutb[:, k, 512 * half:512 * half + 512],
                                                    start=(k == 0), stop=(k == 7)), reads=["ca", f"caL{k // 2}", "woutb"], writes=[B[4 + half]])
            ln_tile(xb, t, 0, None)

        def trev(t):
            for q4 in range(2):
                bt = 6 + q4
                for c in range(4):
                    fc = 4 * q4 + c
                    P.op("pe", lambda pe: pe.transpose(out=banks[bt][:, 128 * c:128 * c + 128], in_=xhb[:, t, 128 * fc:128 * fc + 128],
                                                       identity=identF[:]), reads=[XR(t), "identF"], writes=[B[bt]])
                src = banks[bt][:, :].rearrange("p (c t) -> p c t", c=4)
                if halo_only:
                    P.op("dve", lambda v: v.tensor_scalar(out=hbuf[:, 4 * q4:4 * q4 + 4, 0:2], in0=src[:, :, 126:128],
                                                          scalar1=flg[:, 0:1], scalar2=None, op0=ALU.mult),
                         reads=[B[bt], "flg"], writes=[f"hT{hi}"])
                else:
                    col = hcol0 + 128 * t
                    P.op("act", lambda a: a.activation(out=hbuf[:, 4 * q4:4 * q4 + 4, col:col + 128], in_=src, func=AF.Copy),
                         reads=[B[bt]], writes=[f"hT{hi}"])
        order = []
        for t in range(nt):
            order.append(("m", t))
            if t >= 1:
                order.append(("t", t - 1))
        order.append(("t", nt - 1))
        for kind, t in order:
            pieces.append((lambda t=t: mmln(t)) if kind == "m" else (lambda t=t: trev(t)))
        return pieces

    def ln_tile(xb, t, which, obuf):
        xhb = xh[xb]
        XT = f"xh{xb}_{t}"
        for half in range(2):
            hs_ = slice(512 * half, 512 * half + 512)
            P.op("dve", lambda v: v.scalar_tensor_tensor(out=xhb[:, t, hs_], in0=xhb[:, t, hs_], scalar=ALPHA,
                                                         in1=banks[4 + half][:, :], op0=ALU.mult, op1=ALU.add),
                 reads=[XT, B[4 + half]], writes=[XT])
        for half in range(2):
            hs_ = slice(512 * half, 512 * half + 512)
            P.op("dve", lambda v: v.bn_stats(out=st6[:, 6 * half:6 * half + 6], in_=xhb[:, t, hs_]), reads=[XT], writes=["st6"])
        P.op("dve", lambda v: v.bn_aggr(out=mv[:], in_=st6[:]), reads=["st6"], writes=["mv"])
        P.op("act", lambda a: a.activation(out=rs[:], in_=mv[:, 1:2], func=AF.Sqrt, bias=LN_EPS), reads=["mv"], writes=["rs"])
        P.op("dve", lambda v: v.reciprocal(out=rs[:], in_=rs[:]), reads=["rs"], writes=["rs"])
        P.op("dve", lambda v: v.scalar_tensor_tensor(out=nmr[:], in0=mv[:, 0:1], scalar=-1.0, in1=rs[:], op0=ALU.mult, op1=ALU.mult),
             reads=["mv", "rs"], writes=["nmr"])
        P.op("act", lambda a: a.activation(out=xhb[:, t, :], in_=xhb[:, t, :], func=AF.Identity, scale=rs[:, 0:1], bias=nmr[:, 0:1]),
             reads=[XT, "rs", "nmr"], writes=[XT])
        dst = xhb[:, t, :] if obuf is None else obuf[:]
        dstR = XT if obuf is None else ("ost0" if obuf is ost[0] else "ost1")
        P.op("pool", lambda g: g.tensor_tensor(out=xhb[:, t, :], in0=xhb[:, t, :], in1=lnt[2 * which][:], op=ALU.mult),
             reads=[XT, "lnt"], writes=[XT])
        P.op("dve", lambda v: v.tensor_tensor(out=dst, in0=xhb[:, t, :], in1=lnt[2 * which + 1][:], op=ALU.add),
             reads=[XT, "lnt"], writes=[dstR])

    GL = GROUPS if RUN3 else []
    if RUN3:
        for pc in stage_a(0, -128, 1, 1, 0, 0, halo_only=True):
            pc()
        for pc in stage_a(128 + 128 * GL[0][0], 128 * GL[0][0], GL[0][1], 0, 0, 2):
            pc()
    wslot = 0
    otile = 0
    NWS = len(wupt)
    for gi, (tile0, nt) in enumerate(GL):
        n = 128 * nt
        hb = hT[gi % 2]
        hbR = f"hT{gi % 2}"
        xb = gi % 2
        nxt = []
        if gi + 1 < len(GL):
            nb = hT[(gi + 1) % 2]
            P.op("pool", lambda g: g.tensor_copy(out=nb[:, :, 0:2], in_=hb[:, :, n:n + 2]), reads=[hbR], writes=[f"hT{(gi + 1) % 2}"])
            t0n, ntn = GL[gi + 1]
            nxt = stage_a(128 + 128 * t0n, 128 * t0n, ntn, (gi + 1) % 2, (gi + 1) % 2, 2)
        sched = {0: 1, 4: 1, 7: 1, 10: 1, 12: 1, 15: 1, 18: 1}
        for j in range(NPAIR):
            ws = wslot % NWS
            wslot += 1
            P.dma("sp", lambda q: q.dma_start(out=wupt[ws][:], in_=wup_b[j]), sem(f"wu{ws}"), reads=["wup_b"], writes=[f"wupt{ws}"])
            ab = j % 2
            for hc in range(2):
                bk = (2 * j + hc) % 4
                ch = 22 * hc + j
                for k in range(8):
                    P.op("pe", lambda pe: pe.matmul(banks[bk][:, 0:n + 2], lhsT=wupt[ws][:, hc, k, :], rhs=hb[:, k, 0:n + 2],
                                                    start=(k == 0), stop=(k == 7)), reads=[f"wupt{ws}", hbR], writes=[B[bk]])
                acc = (acc_a if hc == 0 else acc_c)[ab]
                accR = ("acca" if hc == 0 else "accc") + str(ab)
                P.op("act", lambda a: a.activation(out=acc[:, 0:n], in_=banks[bk][:, 2:n + 2], func=AF.Identity,
                                                   scale=cvp[:, ch, 2:3], bias=cvp[:, ch, 3:4]), reads=[B[bk], "cvp"], writes=[accR])
                P.op("dve", lambda v: v.scalar_tensor_tensor(out=acc[:, 0:n], in0=banks[bk][:, 1:n + 1], scalar=cvp[:, ch, 1:2],
                                                             in1=acc[:, 0:n], op0=ALU.mult, op1=ALU.add),
                     reads=[B[bk], "cvp", accR], writes=[accR])
                P.op("dve", lambda v: v.scalar_tensor_tensor(out=acc[:, 0:n], in0=banks[bk][:, 0:n], scalar=cvp[:, ch, 0:1],
                                                             in1=acc[:, 0:n], op0=ALU.mult, op1=ALU.add),
                     reads=[B[bk], "cvp", accR], writes=[accR])
            P.op("act", lambda a: a.activation(out=gel[ab][:, 0:n], in_=acc_a[ab][:, 0:n], func=AF.Gelu),
                 reads=[f"acca{ab}"], writes=[f"gel{ab}"])
            P.op("pool", lambda g: g.tensor_tensor(out=gT[:, j, 0:n], in0=gel[ab][:, 0:n], in1=acc_c[ab][:, 0:n], op=ALU.mult),
                 reads=[f"gel{ab}", f"accc{ab}"], writes=["gT"])
            for _ in range(sched.get(j, 0)):
                if nxt:
                    nxt.pop(0)()
        while nxt:
            nxt.pop(0)()
        for t in range(nt):
            ts_ = slice(128 * t, 128 * t + 128)
            for half in range(2):
                for j in range(NPAIR):
                    P.op("pe", lambda pe: pe.matmul(banks[4 + half][:, :], lhsT=gT[:, j, ts_], rhs=wdownb[:, j, 512 * half:512 * half + 512],
                                                    start=(j == 0), stop=(j == NPAIR - 1)), reads=["gT", "wdownb"], writes=[B[4 + half]])
            ob_ = ost[otile % 2]
            obR = f"ost{otile % 2}"
            otile += 1
            ln_tile(xb, t, 1, ob_)
            r0 = 128 * (tile0 + t)
            P.dma("pool", lambda g: g.dma_start(out=out[r0:r0 + 128, :], in_=ob_[:]), sem("outs"), reads=[obR], writes=["OUT"])
    P.final_wait("sp", ["OUT"])

    def emit(eng_name, eng):
        for waits, fn, semname, inc in P.ops[eng_name]:
            for k, v in waits:
                eng.wait_ge(sems[k], v)
            if fn is None:
                continue
            name, a, k = fn
            ins = getattr(eng, name)(*a, **k)
            ins.then_inc(sems[semname], inc)

    with nc.Block() as block:
        @block.tensor
        def _(pe):
            emit("pe", pe)

        @block.scalar
        def _(a):
            emit("act", a)

        @block.vector
        def _(v):
            emit("dve", v)

        @block.gpsimd
        def _(g):
            emit("pool", g)

        @block.sync
        def _(q):
            emit("sp", q)
    s3.close()
    sW.close()
    es.close()
    return nc


def _consts():
    c = np.zeros((128, K_END), np.float32)
    idx = np.arange(128)
    s_, t_ = idx[:, None], idx[None, :]
    same = (s_ // 64) == (t_ // 64)
    c[:, K_ID:K_ID + 128] = np.eye(128)
    c[:, K_NTRI:K_NTRI + 128] = -1.0 * (s_ >= t_)
    c[:, K_NONE:K_NONE + 128] = -1.0
    c[:, K_MASKL:K_MASKL + 128] = (s_ < t_)
    ref = 64 * (t_ // 64) + 31
    m1 = np.where(t_ >= ref, ((s_ > ref) & (s_ <= t_)).astype(np.float32), -((s_ > t_) & (s_ <= ref)).astype(np.float32))
    c[:, K_M1:K_M1 + 128] = m1 * same
    c[:, K_M3:K_M3 + 128] = same & (s_ <= t_)
    c[:, K_IND:K_IND + 2] = (idx[:, None] // 64) == np.arange(2)[None, :]
    c[:, K_M2:K_M2 + 128] = same & (s_ > t_)
    c[:, K_ONES:K_ONES + 128] = 1.0
    c[:, K_MEAN:K_MEAN + 128] = 1.0 / 128
    c[:, K_MASKL2:K_MASKL2 + 128] = (s_ < t_)
    c[:, K_MASKL2 + 128:K_MASKL2 + 256] = (s_ < t_)
    return c


_NC_CACHE = {}


def kernel(x, w_in, gate_up, gate_bias, gla_norm_g, w_out, ln1_g, ln1_b, w_up, conv_w, conv_b, w_down, ln2_g, ln2_b):
    f = lambda a: np.ascontiguousarray(np.asarray(a, dtype=np.float32))
    x = f(x); w_in = f(w_in)[0]; gate_up = f(gate_up)[0]; gate_bias = f(gate_bias)[0]; gla_norm_g = f(gla_norm_g)[0]
    w_out = f(w_out)[0]; w_up = f(w_up)[0]; conv_w = f(conv_w)[0]; conv_b = f(conv_b)[0]; w_down = f(w_down)[0]
    lnp = np.stack([f(ln1_g)[0], f(ln1_b)[0], f(ln2_g)[0], f(ln2_b)[0]], 0)
    if "nc" not in _NC_CACHE:
        _NC_CACHE["nc"] = build()
    nc = _NC_CACHE["nc"]
    cst = _consts()
    perm = np.concatenate([np.arange(128) + (256 * r + 128 * c if c < 2 else 512 + 256 * r + 128 * (c - 2))
                           for c in range(4) for r in range(2)])
    wout_r = f(w_out[perm].reshape(8, 128, D).transpose(1, 0, 2))
    wup_r = f(w_up.reshape(8, 128, 2, NPAIR, 128).transpose(3, 1, 2, 0, 4))
    convp = np.concatenate([conv_w, conv_b[None, :]], 0)
    convp_r = f(convp.reshape(4, 44, 128).transpose(2, 1, 0))
    wdown_r = f(w_down.reshape(NPAIR, 128, D).transpose(1, 0, 2))
    gng = f(gla_norm_g.reshape(128, 1))
    in_maps = []
    for c in range(NCORE):
        b, g = divmod(c, 2)
        xT = f(x[b].T.reshape(8, 128, 8, 512).transpose(2, 1, 0, 3))
        xo = np.zeros((17 * 128, D), np.float32)
        xo[128:] = x[b, 2048 * g:2048 * g + 2048]
        if g == 1:
            xo[:128] = x[b, 1920:2048]
        cols = np.concatenate([
            np.arange(256 * g, 256 * g + 256), 512 + np.arange(256 * g, 256 * g + 256),
            1536 + np.arange(128 * g, 128 * g + 128), 1792 + np.arange(128 * g, 128 * g + 128),
            2560 + np.arange(256 * g, 256 * g + 256), 3072 + np.arange(16),
            2048 + np.arange(256 * g, 256 * g + 256)])
        w1f = np.zeros((D, W1C), np.float32)
        w1f[:, :C_V] = w_in[:, cols[:C_V]]
        for hl in range(4):
            hh = hl % 2
            w1f[:, C_V + 128 * hl + 64 * hh:C_V + 128 * hl + 64 * hh + 64] = w_in[:, 1024 + 256 * g + 64 * hl:1024 + 256 * g + 64 * hl + 64]
        w1f[:, C_GV:] = w_in[:, cols[C_V:]]
        w1 = f(w1f.reshape(8, 128, W1C).transpose(1, 0, 2))
        fl = np.zeros((128, 2), np.float32)
        fl[:, 0] = g
        fl[:, 1] = 1 - g
        in_maps.append({
            "xT": xT, "xo": xo, "w1": w1, "gup": f(gate_up[:, 128 * g:128 * g + 128]),
            "gbias": f(gate_bias[None, 128 * g:128 * g + 128]), "gng": gng, "wout": wout_r, "lnp": lnp,
            "wup": wup_r, "convp": convp_r, "wdown": wdown_r, "flag": fl, "cst": cst})
    res = run_bass_kernel_spmd(nc, in_maps, core_ids=list(range(NCORE)))
    outp = np.zeros((4, S, D), np.float32)
    for c in range(NCORE):
        b, g = divmod(c, 2)
        outp[b, 2048 * g:2048 * g + 2048] = res.results[c]["out"]
    return outp
```

```python
import contextlib
import numpy as np
import concourse.bass as bass
import concourse.mybir as mybir
from concourse.bass_utils import run_bass_kernel_spmd

F32 = mybir.dt.float32
BF16 = mybir.dt.bfloat16
AF = mybir.ActivationFunctionType
ALU = mybir.AluOpType

D = 1024
S = 4096
NCORE = 8
DFF = 2816
NPAIR = DFF // 128
ALPHA = float((2.0 * 1) ** 0.25)
LN_EPS = 1e-5
RMS_EPS = 1e-6
LN8 = float(np.log(0.125))
W1C = 1808
C_Q, C_K, C_GQ, C_GK, C_GG, C_GA, C_V, C_GV = 0, 256, 512, 640, 768, 1024, 1040, 1552
K_ID, K_NTRI, K_NONE, K_MASKL, K_M1, K_M3, K_IND, K_M2, K_ONES, K_ZERO, K_MEAN, K_MASKL2, K_END = (
    0, 128, 256, 384, 512, 640, 768, 776, 904, 1032, 1160, 1288, 1544)
GRP = 384
GROUPS = [(0, 2), (2, 3), (5, 3), (8, 3), (11, 3), (14, 2)]


class _Rec:
    def __init__(self):
        self.call = None

    def __getattr__(self, name):
        def f(*a, **k):
            self.call = (name, a, k)
        return f


def _record(fn):
    r = _Rec()
    fn(r)
    assert r.call is not None
    return r.call


class Prog:
    ENGS = ("pe", "act", "dve", "pool", "sp")

    def __init__(self):
        self.ops = {e: [] for e in self.ENGS}
        self.count = {e: 0 for e in self.ENGS}
        self.known = {e: {} for e in self.ENGS}
        self.last_w = {}
        self.readers = {}
        self.dma_cnt = {}

    def _waits(self, eng, reads, writes):
        waits = {}

        def need(dep, war=False):
            key, val, deng = dep
            if deng == eng and eng == "pe":
                return
            if waits.get(key, 0) < val:
                waits[key] = val
        for r in reads:
            if r in self.last_w:
                need(self.last_w[r])
            if len(r) == 2 and r[0] == "b" and r[1].isdigit():
                for rd in self.readers.get(r, ()):
                    if rd[2] != eng:
                        need(rd)
        for w in writes:
            if w in self.last_w:
                need(self.last_w[w])
            for rd in self.readers.get(w, ()):
                need(rd, war=True)
        fin = []
        for k, v in waits.items():
            if self.known[eng].get(k, 0) < v:
                self.known[eng][k] = v
                fin.append((k, v))
        return fin

    def _commit(self, me, reads, writes):
        for r in reads:
            self.readers.setdefault(r, []).append(me)
        for w in writes:
            self.last_w[w] = me
            self.readers[w] = []

    def op(self, eng, fn, reads=(), writes=()):
        fin = self._waits(eng, reads, writes)
        self.count[eng] += 1
        me = (eng, self.count[eng], eng)
        self.ops[eng].append((fin, _record(fn), eng, 1))
        self._commit(me, reads, writes)

    def dma(self, queue, fn, sem, reads=(), writes=(), inc=16):
        fin = self._waits(queue, reads, writes)
        self.dma_cnt[sem] = self.dma_cnt.get(sem, 0) + inc
        me = (sem, self.dma_cnt[sem], "dma")
        self.ops[queue].append((fin, _record(fn), sem, inc))
        self._commit(me, reads, writes)

    def final_wait(self, eng, res):
        fin = self._waits(eng, res, ())
        self.ops[eng].append((fin, None, None, 0))


def build():
    import os
    STOP = os.environ.get('K_STOP', 'full')
    RUN2 = STOP not in ('p1',)
    K_NG = int(os.environ.get('K_NG', '8'))
    RG = [[0, 1]] if os.environ.get('K_RG') == '2' else [[0, 1], [2, 3], [4, 5], [6, 7]]
    K_PART = int(os.environ.get('K_PART', '9'))
    K_SKIP = os.environ.get('K_SKIP', '')
    RUNCC = STOP not in ('p1', 'p2')
    RUN3 = STOP not in ('p1', 'p2', 'cc')
    nc = bass.Bass("TRN2", target_bir_lowering=False)
    P = Prog()
    global _LASTP
    _LASTP = P
    es = contextlib.ExitStack()

    def dram(name, shape, dt=F32, kind="ExternalInput"):
        return nc.dram_tensor(name, list(shape), dt, kind=kind).ap()

    xT = dram("xT", [8, 128, 8, 512])
    xo = dram("xo", [17 * 128, D])
    w1 = dram("w1", [128, 8, W1C])
    gup = dram("gup", [16, 128])
    gbias = dram("gbias", [1, 128])
    gng = dram("gng", [128, 1])
    wout = dram("wout", [128, 8, D])
    lnp = dram("lnp", [4, D])
    wup = dram("wup", [NPAIR, 128, 2, 8, 128])
    convp = dram("convp", [128, 44, 4])
    wdown = dram("wdown", [128, NPAIR, D])
    flag = dram("flag", [128, 2])
    cst = dram("cst", [128, K_END])
    out = dram("out", [2048, D], kind="ExternalOutput")
    wup_b = dram("wup_b", [NPAIR, 128, 2, 8, 128], BF16, kind="Internal")
    wout_b = dram("wout_b", [128, 8, D], BF16, kind="Internal")
    wdown_b = dram("wdown_b", [128, NPAIR, D], BF16, kind="Internal")
    mix_src = [dram(f"mix_src{c}", [128, S], BF16, kind="Internal") for c in range(4)]
    mix_ag = [dram(f"mix_ag{c}", [256, S], BF16, kind="Internal") for c in range(4)]
    TSPL = 3456
    mix_src1b = dram("mix_src1b", [128, S - TSPL], BF16, kind="Internal")
    mix_ag1b = dram("mix_ag1b", [256, S - TSPL], BF16, kind="Internal")
    mix_src1a = dram("mix_src1a", [128, TSPL], BF16, kind="Internal")
    mix_ag1a = dram("mix_ag1a", [256, TSPL], BF16, kind="Internal")

    def sb(name, shape, dt, stack=es):
        return stack.enter_context(nc.sbuf_tensor(name, list(shape), dt))

    sems = {}

    def sem(name):
        if name not in sems:
            sems[name] = es.enter_context(nc.semaphore(name))
        return name
    for e in ("pe", "act", "dve", "pool"):
        sem(e)

    bankp = [es.enter_context(nc.psum_tensor(f"bankp{i}", [128, 2, 512], F32)) for i in range(4)]
    banks = [bankp[i // 2][:, i % 2, :] for i in range(8)]
    B = [f"b{i}" for i in range(8)]

    cb = sb("cb", [128, K_END], BF16)
    identF = sb("identF", [128, 128], F32)
    flg = sb("flg", [128, 2], F32)
    gngt = sb("gngt", [128, 1], F32)
    P.dma("pool", lambda g: g.dma_start(out=cb[:], in_=cst), sem("c_cb"), writes=["cb"])
    P.dma("sp", lambda q: q.dma_start(out=identF[:], in_=cst[:, K_ID:K_ID + 128]), sem("c_id"), writes=["identF"])
    P.dma("sp", lambda q: q.dma_start(out=flg[:], in_=flag), sem("c_fl"), writes=["flg"])
    P.dma("sp", lambda q: q.dma_start(out=gngt[:], in_=gng), sem("c_gn"), writes=["gng"])
    ident_b = cb[:, K_ID:K_ID + 128]
    negtri = cb[:, K_NTRI:K_NTRI + 128]
    negones = cb[:, K_NONE:K_NONE + 128]
    maskL = cb[:, K_MASKL:K_MASKL + 128]
    M1 = cb[:, K_M1:K_M1 + 128]
    M3 = cb[:, K_M3:K_M3 + 128]
    IND = cb[:, K_IND:K_IND + 2]
    M2 = cb[:, K_M2:K_M2 + 128]
    ones_b = cb[:, K_ONES:K_ONES + 128]
    zeros_b = cb[:, K_ZERO:K_ZERO + 128]
    mean_b = cb[:, K_MEAN:K_MEAN + 128]
    maskL2 = cb[:, K_MASKL2:K_MASKL2 + 256].rearrange("p (a b) -> p a b", a=2)

    sA = contextlib.ExitStack()
    QT = sb("QT", [128, 2, S], BF16, sA)
    KT = sb("KT", [128, 2, S], BF16, sA)
    Vp = sb("Vp", [128, 32, 4, 128], BF16, sA)
    mixT = sb("mixT", [128, 4, S], BF16, sA)

    def exchange(c, part=None):
        if not RUNCC:
            return
        if part is None:
            src, dst, tsl_, tg_list, tag = mix_src[c], mix_ag[c], slice(0, S), range(8), f"{c}"
        elif part == "a":
            src, dst, tsl_, tg_list, tag = mix_src1a, mix_ag1a, slice(0, TSPL), range(7), "1a"
        else:
            src, dst, tsl_, tg_list, tag = mix_src1b, mix_ag1b, slice(TSPL, S), range(6, 8), "1b"
        P.dma("sp", lambda q: q.dma_start(out=src, in_=mixT[:, c, tsl_]), sem(f"mxs{tag}"),
              reads=[f"mixT{c}_{t}" for t in tg_list], writes=[f"mix_src{tag}"])
        P.dma("pool", lambda g: g.collective_compute("AllGather", ALU.bypass, replica_groups=RG, ins=[src], outs=[dst]), sem(f"cc{tag}"),
              reads=[f"mix_src{tag}"], writes=[f"mix_ag{tag}"], inc=1)

    s1 = contextlib.ExitStack()
    w1b = sb("w1b", [128, 8, W1C], BF16, s1)
    xTb = [sb(f"xTb{i}", [128, 8, 512], BF16, s1) for i in range(2)]
    gupb = sb("gupb", [16, 128], BF16, s1)
    gbb = sb("gbb", [1, 128], BF16, s1)
    gq_sb = [sb(f"gq_sb{i}", [128, 512], F32, s1) for i in range(2)]
    gk_sb = [sb(f"gk_sb{i}", [128, 512], F32, s1) for i in range(2)]
    sg = [[sb(f"sg{i}_{h}", [128, 512], F32, s1) for h in range(2)] for i in range(2)]
    gaT = [sb(f"gaT{i}", [16, 512], BF16, s1) for i in range(2)]
    gv_sb = [sb(f"gv_sb{i}", [128, 4, 256], BF16, s1) for i in range(2)]
    gk_tok = [sb(f"gk_tok{i}", [128, 4, 128], F32, s1) for i in range(2)]
    l_tok = sb("l_tok", [128, 4, 128], BF16, s1)
    E1m = sb("E1m", [128, 512], F32, s1)
    E1p = sb("E1p", [128, 512], F32, s1)
    E3 = sb("E3", [128, 512], F32, s1)
    E2 = sb("E2", [128, 512], F32, s1)
    e_g, rstd_g, t1_g = E3, E1m, E1p
    dec = sb("dec", [128, 8], F32, s1)
    q_in = sb("q_in", [128, 512], BF16, s1)
    k_in = sb("k_in", [128, 512], BF16, s1)
    qb = sb("qb", [128, 512], F32, s1)
    k_dec = sb("k_dec", [128, 4, 128], BF16, s1)
    sc_sb = [sb(f"sc_sb{i}", [128, 128], BF16, s1) for i in range(2)]
    St = sb("St", [128, 128], F32, s1)
    Sb = [sb(f"Sb{i}", [128, 128], BF16, s1) for i in range(2)]
    osq2 = [sb(f"osq{h}", [128, 512], BF16, s1) for h in range(2)]

    def load_x(tg):
        s = tg % 2
        P.dma("pool", lambda g: g.dma_start(out=xTb[s][:], in_=xT[tg]), sem(f"x{s}"), writes=[f"xTb{s}"])

    load_x(0)
    WBLK = [(0, 512), (512, 1040), (1040, W1C)]
    for bi, (c0_, c1_) in enumerate(WBLK):
        P.dma("pool", lambda g: g.dma_start(out=w1b[:, :, c0_:c1_], in_=w1[:, :, c0_:c1_]), sem(f"w1_{bi}"), writes=[f"w1b{bi}"])
    P.dma("pool", lambda g: g.dma_start(out=gupb[:], in_=gup), sem("c_gu"), writes=["gupb"])
    P.dma("pool", lambda g: g.dma_start(out=gbb[:], in_=gbias), sem("c_gb"), writes=["gbb"])
    P.op("dve", lambda v: v.memset(St[:], 0.0), writes=["St"])
    P.op("dve", lambda v: v.memset(Sb[0][:], 0.0), writes=["Sb0"])

    def wres(c0):
        return ["w1b0" if c0 < 512 else ("w1b1" if c0 < 1040 else "w1b2")]

    FB = [3, 4]

    def proj_units(tg):
        s = tg % 2
        xt = xTb[s]
        XR = [f"xTb{s}"]
        t0 = tg * 512
        tsl = slice(t0, t0 + 512)
        gb = tg % 2
        units = []
        fcnt = [0]

        def fm(c0, m, evac):
            def u():
                bank = FB[fcnt[0] % 2]
                fcnt[0] += 1
                for k in range(8):
                    P.op("pe", lambda pe: pe.matmul(banks[bank][0:m, :], lhsT=w1b[:, k, c0:c0 + m], rhs=xt[:, k, :],
                                                    start=(k == 0), stop=(k == 7)), reads=wres(c0) + XR, writes=[B[bank]])
                evac(bank)
            units.append(u)
        for hp in range(2):
            fm(C_Q + 128 * hp, 128, lambda bank, hp=hp: P.op(
                "act", lambda a: a.activation(out=QT[:, hp, tsl], in_=banks[bank][:, :], func=AF.Copy, scale=0.125),
                reads=[B[bank]], writes=[f"QT{hp}_{tg}"]))
        for hp in range(2):
            fm(C_K + 128 * hp, 128, lambda bank, hp=hp: P.op(
                "dve", lambda v: v.tensor_copy(out=KT[:, hp, tsl], in_=banks[bank][:, :]), reads=[B[bank]], writes=[f"KT{hp}_{tg}"]))
        fm(C_GQ, 128, lambda bank: P.op("dve", lambda v: v.tensor_copy(out=gq_sb[gb][:], in_=banks[bank][:, :]),
                                        reads=[B[bank]], writes=[f"gq_sb{gb}"]))
        fm(C_GK, 128, lambda bank: P.op("dve", lambda v: v.tensor_copy(out=gk_sb[gb][:], in_=banks[bank][:, :]),
                                        reads=[B[bank]], writes=[f"gk_sb{gb}"]))
        for h in range(2):
            fm(C_GG + 128 * h, 128, lambda bank, h=h: P.op(
                "act", lambda a: a.activation(out=sg[gb][h][:], in_=banks[bank][:, :], func=AF.Silu), reads=[B[bank]], writes=[f"sg{gb}_{h}"]))
        fm(C_GA, 16, lambda bank: P.op("dve", lambda v: v.tensor_copy(out=gaT[gb][:], in_=banks[bank][0:16, :]),
                                       reads=[B[bank]], writes=[f"gaT{gb}"]))
        for i in range(4):
            blk = tg * 4 + i
            lt = xt[:, :, 128 * i:128 * i + 128]

            def uv(i=i, blk=blk, lt=lt):
                for k in range(8):
                    P.op("pe", lambda pe: pe.matmul(banks[7][:, :], lhsT=lt[:, k, :], rhs=w1b[:, k, C_V:C_V + 512],
                                                    start=(k == 0), stop=(k == 7)), reads=wres(C_V) + XR, writes=[B[7]])
                P.op("dve", lambda v: v.tensor_copy(out=Vp[:, blk, :, :].rearrange("p a b -> p (a b)"), in_=banks[7][:, :]),
                     reads=[B[7]], writes=[f"Vp_{blk}"])
            units.append(uv)

            def ug(i=i, lt=lt):
                bank = FB[fcnt[0] % 2]
                fcnt[0] += 1
                for k in range(8):
                    P.op("pe", lambda pe: pe.matmul(banks[bank][:, 0:256], lhsT=lt[:, k, :], rhs=w1b[:, k, C_GV:C_GV + 256],
                                                    start=(k == 0), stop=(k == 7)), reads=wres(C_GV) + XR, writes=[B[bank]])
                for k in range(8):
                    P.op("pe", lambda pe: pe.matmul(banks[bank][:, 256:384], lhsT=lt[:, k, :], rhs=w1b[:, k, C_GK:C_GK + 128],
                                                    start=(k == 0), stop=(k == 7)), reads=wres(C_GK) + XR, writes=[B[bank]])
                P.op("act", lambda a: a.activation(out=gv_sb[gb][:, i, :], in_=banks[bank][:, 0:256], func=AF.Copy),
                     reads=[B[bank]], writes=[f"gv_sb{gb}"])
                P.op("act", lambda a: a.activation(out=gk_tok[gb][:, i, :], in_=banks[bank][:, 256:384], func=AF.Copy),
                     reads=[B[bank]], writes=[f"gk_tok{gb}"])
            units.append(ug)
        return units

    sb_idx = [0]

    def gla_stages(tg):
        gb = tg % 2
        t0 = tg * 512
        tsl = slice(t0, t0 + 512)
        GQ, GK, GV, GKT, GA = f"gq_sb{gb}", f"gk_sb{gb}", f"gv_sb{gb}", f"gk_tok{gb}", f"gaT{gb}"
        st = []

        def s_gate():
            for i in range(4):
                P.op("pe", lambda pe: pe.matmul(banks[5][:, 128 * i:128 * i + 128], lhsT=gaT[gb][:, 128 * i:128 * i + 128],
                                                rhs=gupb[:, :], start=True, stop=False), reads=[GA, "gupb"], writes=[B[5]])
                P.op("pe", lambda pe: pe.matmul(banks[5][:, 128 * i:128 * i + 128], lhsT=ones_b[0:1, :], rhs=gbb[:, :],
                                                start=False, stop=True), reads=["cb", "gbb"], writes=[B[5]])
            P.op("act", lambda a: a.activation(out=e_g[:], in_=banks[5][:, :], func=AF.Exp, scale=-1.0), reads=[B[5]], writes=["E3"])
            P.op("act", lambda a: a.activation(out=l_tok[:, :, :].rearrange("p a b -> p (a b)"), in_=e_g[:], func=AF.Ln, bias=1.0),
                 reads=["E3"], writes=["l_tok"])
        st.append(s_gate)

        def s_cum1():
            for i in range(4):
                P.op("pe", lambda pe: pe.matmul(banks[2][:, 128 * i:128 * i + 128], lhsT=l_tok[:, i, :], rhs=M1, start=True, stop=True),
                     reads=["l_tok", "cb"], writes=[B[2]])
                P.op("pe", lambda pe: pe.matmul(banks[6][:, 256 + 2 * i:258 + 2 * i], lhsT=l_tok[:, i, :], rhs=IND, start=True, stop=True),
                     reads=["l_tok", "cb"], writes=[B[6]])
            P.op("act", lambda a: a.activation(out=E1m[:], in_=banks[2][:, :], func=AF.Exp, scale=-1.0 / 16, bias=LN8), reads=[B[2]], writes=["E1m"])
            P.op("act", lambda a: a.activation(out=E1p[:], in_=banks[2][:, :], func=AF.Exp, scale=1.0 / 16), reads=[B[2]], writes=["E1p"])
            P.op("act", lambda a: a.activation(out=dec[:], in_=banks[6][:, 256:264], func=AF.Exp, scale=-1.0 / 16), reads=[B[6]], writes=["dec"])
            P.op("pool", lambda g: g.tensor_tensor(out=q_in[:], in0=gq_sb[gb][:], in1=E1m[:], op=ALU.mult), reads=[GQ, "E1m"], writes=["q_in"])
            P.op("dve", lambda v: v.tensor_tensor(out=k_in[:], in0=gk_sb[gb][:], in1=E1p[:], op=ALU.mult), reads=[GK, "E1p"], writes=["k_in"])
        st.append(s_cum1)

        def s_cum2():
            for i in range(4):
                P.op("pe", lambda pe: pe.matmul(banks[2][:, 128 * i:128 * i + 128], lhsT=l_tok[:, i, :], rhs=M3, start=True, stop=True),
                     reads=["l_tok", "cb"], writes=[B[2]])
            P.op("act", lambda a: a.activation(out=E3[:], in_=banks[2][:, :], func=AF.Exp, scale=-1.0 / 16, bias=LN8), reads=[B[2]], writes=["E3"])
            P.op("pool", lambda g: g.tensor_tensor(out=qb[:], in0=gq_sb[gb][:], in1=E3[:], op=ALU.mult), reads=[GQ, "E3"], writes=["qb"])
        st.append(s_cum2)

        def s_cum3():
            for i in range(4):
                P.op("pe", lambda pe: pe.matmul(banks[2][:, 128 * i:128 * i + 128], lhsT=M2, rhs=l_tok[:, i, :], start=True, stop=True),
                     reads=["l_tok", "cb"], writes=[B[2]])
            P.op("act", lambda a: a.activation(out=E2[:], in_=banks[2][:, :], func=AF.Exp, scale=-1.0 / 16), reads=[B[2]], writes=["E2"])
            P.op("dve", lambda v: v.tensor_tensor(out=k_dec[:, :, :].rearrange("p a b -> p (a b)"),
                                                  in0=gk_tok[gb][:, :, :].rearrange("p a b -> p (a b)"), in1=E2[:], op=ALU.mult),
                 reads=[GKT, "E2"], writes=["k_dec"])
        st.append(s_cum3)
        OB = [0, 1]
        for i in range(4):
            cs = slice(128 * i, 128 * i + 128)

            def s_scores(i=i, cs=cs):
                for h in range(2):
                    hs = slice(64 * h, 64 * h + 64)
                    si = h
                    sbk = [5, 2][h]
                    P.op("pe", lambda pe: pe.matmul(banks[sbk][:, 0:128], lhsT=k_in[hs, cs], rhs=q_in[hs, cs], start=True, stop=True),
                         reads=["k_in", "q_in"], writes=[B[sbk]])
                for h in range(2):
                    si = h
                    sbk = [5, 2][h]
                    P.op("dve", lambda v: v.tensor_tensor(out=sc_sb[si][:], in0=banks[sbk][:, 0:128], in1=M3, op=ALU.mult),
                         reads=[B[sbk], "cb"], writes=[f"sc{si}"])
            st.append(s_scores)

            def s_intra(i=i, cs=cs):
                for h in range(2):
                    P.op("pe", lambda pe: pe.matmul(banks[OB[h]][:, cs], lhsT=gv_sb[gb][:, i, 128 * h:128 * h + 128], rhs=sc_sb[h][:],
                                                    start=True, stop=False), reads=[GV, f"sc{h}"], writes=[B[OB[h]]])
            st.append(s_intra)
            for c in range(2):
                def s_chunk(i=i, c=c):
                    ci = 2 * i + c
                    ccs = slice(128 * i + 64 * c, 128 * i + 64 * c + 64)
                    rows = slice(64 * c, 64 * c + 64)
                    for h in range(2):
                        hs = slice(64 * h, 64 * h + 64)
                        P.op("pe", lambda pe: pe.matmul(banks[OB[h]][:, ccs], lhsT=St[hs, :], rhs=qb[hs, ccs], start=False, stop=(c == 1)),
                             reads=["St", "qb"], writes=[B[OB[h]]])
                    P.op("pe", lambda pe: pe.matmul(banks[6][:, 0:256], lhsT=k_dec[rows, i, :], rhs=gv_sb[gb][rows, i, :], start=True, stop=True),
                         reads=["k_dec", GV], writes=[B[6]])
                    for h in range(2):
                        hs = slice(64 * h, 64 * h + 64)
                        P.op("dve", lambda v: v.scalar_tensor_tensor(out=St[hs, :], in0=St[hs, :], scalar=dec[hs, ci:ci + 1],
                                                                     in1=banks[6][hs, 128 * h:128 * h + 128], op0=ALU.mult, op1=ALU.add),
                             reads=["St", "dec", B[6]], writes=["St"])
                st.append(s_chunk)
        def s_out():
            rbuf, rtok = [E1m, E3], ["E1m", "E3"]
            tbuf, ttok = [E1p, E2], ["E1p", "E2"]
            mbank = [5, 2]
            for h in range(2):
                P.op("act", lambda a: a.activation(out=osq2[h][:], in_=banks[OB[h]][:, :], func=AF.Square), reads=[B[OB[h]]], writes=[f"osq{h}"])
                P.op("pe", lambda pe: pe.matmul(banks[mbank[h]][:, :], lhsT=mean_b, rhs=osq2[h][:], start=True, stop=True),
                     reads=[f"osq{h}", "cb"], writes=[B[mbank[h]]])
            for h in range(2):
                P.op("act", lambda a: a.activation(out=rbuf[h][:], in_=banks[mbank[h]][:, :], func=AF.Ln, bias=RMS_EPS),
                     reads=[B[mbank[h]]], writes=[rtok[h]])
                P.op("act", lambda a: a.activation(out=rbuf[h][:], in_=rbuf[h][:], func=AF.Exp, scale=-0.5), reads=[rtok[h]], writes=[rtok[h]])
            for h in range(2):
                P.op("dve", lambda v: v.scalar_tensor_tensor(out=tbuf[h][:], in0=banks[OB[h]][:, :], scalar=gngt[:, 0:1], in1=rbuf[h][:],
                                                             op0=ALU.mult, op1=ALU.mult), reads=[B[OB[h]], "gng", rtok[h]], writes=[ttok[h]])
            for h in range(2):
                P.op("pool", lambda g: g.tensor_tensor(out=mixT[:, 2 + h, tsl], in0=tbuf[h][:], in1=sg[gb][h][:], op=ALU.mult),
                     reads=[ttok[h], f"sg{gb}_{h}"], writes=[f"mixT{2 + h}_{tg}"])
        st.append(s_out)
        return st

    for u in proj_units(0)[:int(os.environ.get('K_PU', '99'))]:
        u()
    bg = []
    if RUN3:
        for k in range(4):
            bg.append(lambda k=k: P.dma("pool", lambda g: g.dma_start(out=wout_b[:, 2 * k:2 * k + 2, :], in_=wout[:, 2 * k:2 * k + 2, :]),
                                        sem("woc"), writes=["wout_b"]))
        for k in range(11):
            bg.append(lambda k=k: P.dma("pool", lambda g: g.dma_start(out=wdown_b[:, 2 * k:2 * k + 2, :], in_=wdown[:, 2 * k:2 * k + 2, :]),
                                        sem("wdc"), writes=["wdown_b"]))
    if RUN2:
        for j in range(NPAIR):
            bg.append(lambda j=j: P.dma("pool", lambda g: g.dma_start(out=wup_b[j], in_=wup[j]), sem("wupc"), writes=["wup_b"]))
    load_x(1)
    for tg in range(K_NG):
        if tg + 2 < 8:
            load_x(tg + 2)
        pu = proj_units(tg + 1) if tg + 1 < K_NG else []
        gs = gla_stages(tg)[:int(os.environ.get('K_GS', '99'))]
        npu, ngs = len(pu), len(gs)
        done = 0
        skip = {5, 9, 13, 17}
        for k in range(ngs):
            gs[k]()
            if k not in skip and done < npu:
                pu[done]()
                done += 1
        while done < npu:
            pu[done]()
            done += 1
    exchange(2)
    exchange(3)
    P.op("pool", lambda g: g.memset(q_in[:, 0:1], 0.0),
         reads=["gq_sb0", "gq_sb1", "gk_sb0", "gk_sb1", "sg0_0", "sg0_1", "sg1_0", "sg1_1", "gaT0", "gaT1", "gv_sb0", "gv_sb1",
                "gk_tok0", "gk_tok1", "l_tok", "E1m", "E1p", "E3", "E2", "dec", "q_in", "k_in", "qb", "k_dec", "sc0", "sc1", "St",
                "Sb0", "Sb1", "osq0", "osq1", "w1b0", "w1b1", "w1b2", "xTb0", "xTb1", "gupb", "gbb"], writes=["P1DONE"])
    s1.close()
    sW = contextlib.ExitStack()
    woutb = sW.enter_context(nc.sbuf_tensor("woutb", [128, 8, D], BF16, side="right"))
    wdownb = sW.enter_context(nc.sbuf_tensor("wdownb", [128, NPAIR, D], BF16, side="right"))
    lnt = [sW.enter_context(nc.sbuf_tensor(f"lnt{i}", [128, D], F32, side="right")) for i in range(4)]
    cvp = sW.enter_context(nc.sbuf_tensor("cvp", [128, 44, 4], F32, side="right"))
    s2 = contextlib.ExitStack()
    e_buf = [sb(f"e_buf{i}", [128, 2, 512], F32, s2) for i in range(2)]
    sp_buf = [sb(f"sp_buf{i}", [128, 2, 512], BF16, s2) for i in range(3)]
    w_buf = [sb(f"w_buf{i}", [128, 2, 512], BF16, s2) for i in range(3)]
    Rf = sb("Rf", [128, 2, 512], F32, s2)
    Rb = [sb(f"Rb{j}", [128, 2, 512], BF16, s2) for j in range(2)]
    P2RES = ["e0", "e1", "sp0", "sp1", "sp2", "wb0", "wb1", "wb2", "Rf", "Rb0", "Rb1"]
    P.op("pool", lambda g: g.memset(e_buf[0][:, 0, 0:1], 0.0), reads=["P1DONE"], writes=P2RES)
    wup_issued = 0
    NE, NSP, NW = 2, 3, 3
    tiles = []
    for hp in range(2 if RUN2 else 0):
        for G in range(8):
            ntile = 4 * G + 4
            for i in range(ntile - 1, -1, -1):
                tiles.append(dict(hp=hp, G=G, i=i, first=(i == ntile - 1), gend=(i == 0), idx=len(tiles)))
    rbi = 0
    for T in tiles:
        if T["first"]:
            rbi = 0
        T["rb_read"] = rbi % 2
        if T["i"] > 0:
            rbi += 1
            T["rb_write"] = rbi % 2
    ZA = bankp[0]
    ZB = [bankp[1], bankp[2]]
    ZAR = [B[0], B[1]]
    ZBR = [[B[2], B[3]], [B[4], B[5]]]

    def load_resident():
        for i in range(4):
            P.dma("sp", lambda q: q.dma_start(out=lnt[i][:], in_=lnp[i:i + 1, :].partition_broadcast(128)), sem("p3c"),
                  reads=["P1DONE"], writes=["lnt"])
        P.dma("sp", lambda q: q.dma_start(out=cvp[:], in_=convp), sem("p3cv"), reads=["P1DONE"], writes=["cvp"])
        P.dma("sp", lambda q: q.dma_start(out=woutb[:], in_=wout_b), sem("p3wo"), reads=["P1DONE", "wout_b"], writes=["woutb"])
        P.dma("sp", lambda q: q.dma_start(out=wdownb[:, 0:11, :], in_=wdown_b[:, 0:11, :]), sem("p3wd"), reads=["P1DONE", "wdown_b"], writes=["wdownb"])
        P.dma("sp", lambda q: q.dma_start(out=wdownb[:, 11:22, :], in_=wdown_b[:, 11:22, :]), sem("p3wd"), reads=["P1DONE", "wdown_b"], writes=["wdownb"])

    def geom(T):
        r = T["i"] - 4 * T["G"]
        c0 = max(0, 128 * r)
        return r, c0, slice(c0, 512), 512 * T["G"]

    def S1(T):
        nonlocal wup_issued
        hp, G, i, n = T["hp"], T["G"], T["i"], T["idx"]
        r, c0, cs, qsl0 = geom(T)
        ob = 6 + ((hp * 8 + G) % 2)
        if T["first"]:
            P.op("pe", lambda pe: pe.matmul(banks[ob][:, :], lhsT=zeros_b, rhs=QT[:, hp, qsl0:qsl0 + 512], start=True, stop=False),
                 reads=["cb", f"QT{hp}_{G}"], writes=[B[ob]])
            P.op("dve", lambda v: v.memset(Rf[:, :, :], 0.0), writes=["Rf"])
            for _ in range(2 if hp == 0 else 3):
                if bg:
                    bg.pop(0)()
            if hp == 1 and G == 0 and RUN3:
                load_resident()
        for hh in range(2):
            pb = slice(64 * hh, 64 * hh + 64)
            P.op("pe", lambda pe: pe.matmul(ZA[:, hh, cs], lhsT=KT[pb, hp, 128 * i:128 * i + 128], rhs=QT[pb, hp, qsl0 + c0:qsl0 + 512],
                                            start=True, stop=True), reads=[f"KT{hp}_{i // 4}", f"QT{hp}_{G}"], writes=[ZAR[hh]])

    def S2a(T):
        n = T["idx"]
        r, c0, cs, qsl0 = geom(T)
        eb = n % NE
        P.op("act", lambda a: a.activation(out=e_buf[eb][:, :, cs], in_=ZA[:, :, cs], func=AF.Exp), reads=ZAR, writes=[f"e{eb}"])

    def S2b(T):
        i, n = T["i"], T["idx"]
        r, c0, cs, qsl0 = geom(T)
        eb, spb = n % NE, n % NSP
        P.op("act", lambda a: a.activation(out=sp_buf[spb][:, :, cs], in_=e_buf[eb][:, :, cs], func=AF.Ln, bias=1.0),
             reads=[f"e{eb}"], writes=[f"sp{spb}"])
        if r >= 0:
            ds = slice(128 * r, 128 * r + 128)
            P.op("dve", lambda v: v.tensor_tensor(out=sp_buf[spb][:, :, ds], in0=sp_buf[spb][:, :, ds], in1=maskL2, op=ALU.mult),
                 reads=[f"sp{spb}", "cb"], writes=[f"sp{spb}"])
        if i > 0:
            P.op("dve", lambda v: v.tensor_tensor(out=Rf[:, :, cs], in0=Rf[:, :, cs], in1=sp_buf[spb][:, :, cs], op=ALU.add),
                 reads=["Rf", f"sp{spb}"], writes=["Rf"])
            rbn = Rb[T["rb_write"]]
            P.op("dve", lambda v: v.tensor_copy(out=rbn[:, :, :], in_=Rf[:, :, :]), reads=["Rf"], writes=[f"Rb{T['rb_write']}"])

    def S3a(T):
        hp, G, i, n = T["hp"], T["G"], T["i"], T["idx"]
        r, c0, cs, qsl0 = geom(T)
        zb, spb = n % 2, n % NSP
        first = T["first"]
        for hh in range(2):
            pb = slice(64 * hh, 64 * hh + 64)
            P.op("pe", lambda pe: pe.matmul(ZB[zb][:, hh, cs], lhsT=KT[pb, hp, 128 * i:128 * i + 128], rhs=QT[pb, hp, qsl0 + c0:qsl0 + 512],
                                            start=True, stop=False), reads=[f"KT{hp}_{i // 4}", f"QT{hp}_{G}"], writes=[ZBR[zb][hh]])
            P.op("pe", lambda pe: pe.matmul(ZB[zb][:, hh, cs], lhsT=negtri, rhs=sp_buf[spb][:, hh, cs], start=False, stop=first),
                 reads=[f"sp{spb}", "cb"], writes=[ZBR[zb][hh]])
            if not first:
                rb = Rb[T["rb_read"]]
                P.op("pe", lambda pe: pe.matmul(ZB[zb][:, hh, cs], lhsT=negones, rhs=rb[:, hh, cs], start=False, stop=True),
                     reads=[f"Rb{T['rb_read']}", "cb"], writes=[ZBR[zb][hh]])

    def S3b(T):
        n = T["idx"]
        r, c0, cs, qsl0 = geom(T)
        zb, wb = n % 2, n % NW
        P.op("act", lambda a: a.activation(out=w_buf[wb][:, :, cs], in_=ZB[zb][:, :, cs], func=AF.Exp), reads=ZBR[zb], writes=[f"wb{wb}"])
        if r >= 0:
            ds = slice(128 * r, 128 * r + 128)
            P.op("dve", lambda v: v.tensor_tensor(out=w_buf[wb][:, :, ds], in0=w_buf[wb][:, :, ds], in1=maskL2, op=ALU.mult),
                 reads=[f"wb{wb}", "cb"], writes=[f"wb{wb}"])

    def S4(T):
        hp, G, i, n = T["hp"], T["G"], T["i"], T["idx"]
        r, c0, cs, qsl0 = geom(T)
        ob = 6 + ((hp * 8 + G) % 2)
        wb = n % NW
        for hh in range(2):
            hl = 2 * hp + hh
            P.op("pe", lambda pe: pe.matmul(banks[ob][:, cs], lhsT=Vp[:, i, hl, :], rhs=w_buf[wb][:, hh, cs], start=False,
                                            stop=(T["gend"] and hh == 1)),
                 reads=[f"Vp_{i}", f"wb{wb}"], writes=[B[ob]])
        if T["gend"]:
            P.op("dve", lambda v: v.tensor_copy(out=mixT[:, hp, qsl0:qsl0 + 512], in_=banks[ob][:, :]),
                 reads=[B[ob]], writes=[f"mixT{hp}_{G}"])
            if hp == 0 and G == 7:
                exchange(0)
            if hp == 1 and G == 6:
                exchange(1, "a")

    NT = len(tiles)
    for s_ in range(NT + 3):
        if s_ < NT:
            S1(tiles[s_])
            S2a(tiles[s_])
        if 0 <= s_ - 1 < NT:
            S3a(tiles[s_ - 1])
        if 0 <= s_ - 2 < NT:
            S3b(tiles[s_ - 2])
        if s_ < NT:
            S2b(tiles[s_])
        if 0 <= s_ - 3 < NT:
            S4(tiles[s_ - 3])
    while bg:
        bg.pop(0)()
    exchange(1, "b")
    allmix = [f"mixT{c}_{t}" for c in range(4) for t in range(8)]
    P.op("pool", lambda g: g.memset(e_buf[0][:, 0, 0:1], 0.0), reads=P2RES + ([f"mix_src{c}" for c in ("0", "1a", "1b", "2", "3")] if RUNCC else []) + [f"Vp_{b_}" for b_ in range(32)] + allmix, writes=["P2DONE"])
    s2.close()
    sA.close()

    s3 = contextlib.ExitStack()
    ca = sb("ca", [128, 8, GRP], BF16, s3)
    cbb = sb("cbb", [128, 8, GRP], BF16, s3)
    xh = [sb(f"xh{i}", [128, 3, D], F32, s3) for i in range(2)]
    hT = [sb(f"hT{i}", [128, 8, GRP + 2], BF16, s3) for i in range(2)]
    wupt = [sb(f"wupt{i}", [128, 2, 8, 128], BF16, s3) for i in range(8)]
    acc_a = [sb(f"acc_a{i}", [128, GRP], F32, s3) for i in range(2)]
    acc_c = [sb(f"acc_c{i}", [128, GRP], F32, s3) for i in range(2)]
    gel = [sb(f"gel{i}", [128, GRP], F32, s3) for i in range(2)]
    ctmp = [sb(f"ctmp{i}", [128, GRP], F32, s3) for i in range(2)]
    gT = sb("gT", [128, NPAIR, GRP], BF16, s3)
    ost = [sb(f"ost{i}", [128, D], F32, s3) for i in range(2)]
    st6 = sb("st6", [128, 12], F32, s3)
    mv = sb("mv", [128, 2], F32, s3)
    rs = sb("rs", [128, 1], F32, s3)
    nmr = sb("nmr", [128, 1], F32, s3)
    P3RES = [ "ca", "cbb", "caL0", "caL1", "caL2", "caL3", "cbbL0", "cbbL1", "cbbL2", "cbbL3", "xh0_0", "xh0_1", "xh0_2", "xh1_0", "xh1_1", "xh1_2", "hT0", "hT1", "wupt0", "wupt1", "wupt2", "wupt3", "wupt4", "wupt5", "wupt6", "wupt7",
             "acca0", "acca1", "accc0", "accc1", "gel0", "gel1", "ctmp0", "ctmp1", "gT", "ost0", "ost1", "st6", "mv", "rs", "nmr"]
    P.op("pool", lambda g: g.memset(st6[:, 0:1], 0.0), reads=["P2DONE"], writes=P3RES + ["P3GO"])
    def stage_a(xrow0, cand_t0, nt, xb, hi, hcol0, halo_only=False):
        n = 128 * nt
        hbuf = hT[hi]
        xhb = xh[xb]
        XR = lambda t: f"xh{xb}_{t}"
        pieces = []

        def prep():
            a0 = max(cand_t0, 0)
            b0 = 2048 + cand_t0
            for c in range(4):
                for (dstt, t_, semn, tok) in ((ca, a0, "canda", "ca"), (cbb, b0, "candb", "cbb")):
                    if c != 1:
                        srcap, off, tag = mix_ag[c], t_, f"{c}"
                    elif t_ + n <= TSPL:
                        srcap, off, tag = mix_ag1a, t_, "1a"
                    else:
                        assert t_ >= TSPL
                        srcap, off, tag = mix_ag1b, t_ - TSPL, "1b"
                    mv_ = srcap.rearrange("(r p) t -> p r t", p=128)
                    P.dma("sp", lambda q: q.dma_start(out=dstt[:, 2 * c:2 * c + 2, 0:n], in_=mv_[:, :, off:off + n]), sem(semn),
                          reads=[f"mix_ag{tag}"], writes=[f"{tok}L{c}"])
            P.dma("pool", lambda g: g.dma_start(out=xhb[:, 0:nt, :], in_=xo[xrow0:xrow0 + n, :].rearrange("(a p) d -> p a d", p=128)),
                  sem(f"xin{xb}"), writes=[XR(t) for t in range(nt)])
            CAL = [f"caL{c}" for c in range(4)]
            CBL = [f"cbbL{c}" for c in range(4)]
            P.op("dve", lambda v: v.tensor_scalar(out=cbb[:, :, 0:n], in0=cbb[:, :, 0:n], scalar1=flg[:, 0:1], scalar2=None, op0=ALU.mult),
                 reads=CBL + ["flg"], writes=["cbb"] + CBL)
            P.op("dve", lambda v: v.scalar_tensor_tensor(out=ca[:, :, 0:n], in0=ca[:, :, 0:n], scalar=flg[:, 1:2], in1=cbb[:, :, 0:n],
                                                         op0=ALU.mult, op1=ALU.add), reads=CAL + ["cbb", "flg"], writes=["ca"] + CAL)
        pieces.append(prep)

        def mmln(t):
            ts_ = slice(128 * t, 128 * t + 128)
            for half in range(2):
                for k in range(8):
                    P.op("pe", lambda pe: pe.matmul(banks[4 + half][:, :], lhsT=ca[:, k, ts_], rhs=woutb[:, k, 512 * half:512 * half + 512],
                                                    start=(k == 0), stop=(k == 7)), reads=["ca", f"caL{k // 2}", "woutb"], writes=[B[4 + half]])
            ln_tile(xb, t, 0, None)

        def trev(t):
            for q4 in range(2):
                bt = 6 + q4
                for c in range(4):
                    fc = 4 * q4 + c
                    P.op("pe", lambda pe: pe.transpose(out=banks[bt][:, 128 * c:128 * c + 128], in_=xhb[:, t, 128 * fc:128 * fc + 128],
                                                       identity=identF[:]), reads=[XR(t), "identF"], writes=[B[bt]])
                src = banks[bt][:, :].rearrange("p (c t) -> p c t", c=4)
                if halo_only:
                    P.op("dve", lambda v: v.tensor_scalar(out=hbuf[:, 4 * q4:4 * q4 + 4, 0:2], in0=src[:, :, 126:128],
                                                          scalar1=flg[:, 0:1], scalar2=None, op0=ALU.mult),
                         reads=[B[bt], "flg"], writes=[f"hT{hi}"])
                else:
                    col = hcol0 + 128 * t
                    P.op("act", lambda a: a.activation(out=hbuf[:, 4 * q4:4 * q4 + 4, col:col + 128], in_=src, func=AF.Copy),
                         reads=[B[bt]], writes=[f"hT{hi}"])
        order = []
        for t in range(nt):
            order.append(("m", t))
            if t >= 1:
                order.append(("t", t - 1))
        order.append(("t", nt - 1))
        for kind, t in order:
            pieces.append((lambda t=t: mmln(t)) if kind == "m" else (lambda t=t: trev(t)))
        return pieces

    def ln_tile(xb, t, which, obuf):
        xhb = xh[xb]
        XT = f"xh{xb}_{t}"
        for half in range(2):
            hs_ = slice(512 * half, 512 * half + 512)
            P.op("dve", lambda v: v.scalar_tensor_tensor(out=xhb[:, t, hs_], in0=xhb[:, t, hs_], scalar=ALPHA,
                                                         in1=banks[4 + half][:, :], op0=ALU.mult, op1=ALU.add),
                 reads=[XT, B[4 + half]], writes=[XT])
        for half in range(2):
            hs_ = slice(512 * half, 512 * half + 512)
            P.op("dve", lambda v: v.bn_stats(out=st6[:, 6 * half:6 * half + 6], in_=xhb[:, t, hs_]), reads=[XT], writes=["st6"])
        P.op("dve", lambda v: v.bn_aggr(out=mv[:], in_=st6[:]), reads=["st6"], writes=["mv"])
        P.op("act", lambda a: a.activation(out=rs[:], in_=mv[:, 1:2], func=AF.Sqrt, bias=LN_EPS), reads=["mv"], writes=["rs"])
        P.op("dve", lambda v: v.reciprocal(out=rs[:], in_=rs[:]), reads=["rs"], writes=["rs"])
        P.op("dve", lambda v: v.scalar_tensor_tensor(out=nmr[:], in0=mv[:, 0:1], scalar=-1.0, in1=rs[:], op0=ALU.mult, op1=ALU.mult),
             reads=["mv", "rs"], writes=["nmr"])
        P.op("act", lambda a: a.activation(out=xhb[:, t, :], in_=xhb[:, t, :], func=AF.Identity, scale=rs[:, 0:1], bias=nmr[:, 0:1]),
             reads=[XT, "rs", "nmr"], writes=[XT])
        dst = xhb[:, t, :] if obuf is None else obuf[:]
        dstR = XT if obuf is None else ("ost0" if obuf is ost[0] else "ost1")
        P.op("pool", lambda g: g.tensor_tensor(out=xhb[:, t, :], in0=xhb[:, t, :], in1=lnt[2 * which][:], op=ALU.mult),
             reads=[XT, "lnt"], writes=[XT])
        P.op("pool", lambda g: g.tensor_tensor(out=dst, in0=xhb[:, t, :], in1=lnt[2 * which + 1][:], op=ALU.add),
             reads=[XT, "lnt"], writes=[dstR])

    GL = GROUPS if RUN3 else []
    if RUN3:
        for pc in stage_a(0, -128, 1, 1, 0, 0, halo_only=True):
            pc()
        for pc in stage_a(128 + 128 * GL[0][0], 128 * GL[0][0], GL[0][1], 0, 0, 2):
            pc()
    wslot = 0
    otile = 0
    NWS = len(wupt)
    for gi, (tile0, nt) in enumerate(GL):
        n = 128 * nt
        hb = hT[gi % 2]
        hbR = f"hT{gi % 2}"
        xb = gi % 2
        nxt = []
        if gi + 1 < len(GL):
            nb = hT[(gi + 1) % 2]
            P.op("pool", lambda g: g.tensor_copy(out=nb[:, :, 0:2], in_=hb[:, :, n:n + 2]), reads=[hbR], writes=[f"hT{(gi + 1) % 2}"])
            t0n, ntn = GL[gi + 1]
            nxt = stage_a(128 + 128 * t0n, 128 * t0n, ntn, (gi + 1) % 2, (gi + 1) % 2, 2)
        sched = {0: 1, 4: 1, 7: 1, 10: 1, 12: 1, 15: 1, 18: 1}
        for j in range(NPAIR):
            ws = wslot % NWS
            wslot += 1
            P.dma("sp", lambda q: q.dma_start(out=wupt[ws][:], in_=wup_b[j]), sem(f"wu{ws}"), reads=["wup_b"], writes=[f"wupt{ws}"])
            ab = j % 2
            for hc in range(2):
                bk = (2 * j + hc) % 4
                ch = 22 * hc + j
                for k in range(8):
                    P.op("pe", lambda pe: pe.matmul(banks[bk][:, 0:n + 2], lhsT=wupt[ws][:, hc, k, :], rhs=hb[:, k, 0:n + 2],
                                                    start=(k == 0), stop=(k == 7)), reads=[f"wupt{ws}", hbR], writes=[B[bk]])
                acc = (acc_a if hc == 0 else acc_c)[ab]
                accR = ("acca" if hc == 0 else "accc") + str(ab)
                P.op("act", lambda a: a.activation(out=acc[:, 0:n], in_=banks[bk][:, 2:n + 2], func=AF.Identity,
                                                   scale=cvp[:, ch, 2:3], bias=cvp[:, ch, 3:4]), reads=[B[bk], "cvp"], writes=[accR])
                if hc == 0:
                    P.op("dve", lambda v: v.scalar_tensor_tensor(out=acc[:, 0:n], in0=banks[bk][:, 1:n + 1], scalar=cvp[:, ch, 1:2],
                                                                 in1=acc[:, 0:n], op0=ALU.mult, op1=ALU.add),
                         reads=[B[bk], "cvp", accR], writes=[accR])
                else:
                    P.op("act", lambda a: a.activation(out=ctmp[ab][:, 0:n], in_=banks[bk][:, 1:n + 1], func=AF.Copy, scale=cvp[:, ch, 1:2]),
                         reads=[B[bk], "cvp"], writes=[f"ctmp{ab}"])
                P.op("dve", lambda v: v.scalar_tensor_tensor(out=acc[:, 0:n], in0=banks[bk][:, 0:n], scalar=cvp[:, ch, 0:1],
                                                             in1=acc[:, 0:n], op0=ALU.mult, op1=ALU.add),
                     reads=[B[bk], "cvp", accR], writes=[accR])
            P.op("act", lambda a: a.activation(out=gel[ab][:, 0:n], in_=acc_a[ab][:, 0:n], func=AF.Gelu),
                 reads=[f"acca{ab}"], writes=[f"gel{ab}"])
            P.op("pool", lambda g: g.tensor_tensor(out=acc_c[ab][:, 0:n], in0=acc_c[ab][:, 0:n], in1=ctmp[ab][:, 0:n], op=ALU.add),
                 reads=[f"accc{ab}", f"ctmp{ab}"], writes=[f"accc{ab}"])
            P.op("pool", lambda g: g.tensor_tensor(out=gT[:, j, 0:n], in0=gel[ab][:, 0:n], in1=acc_c[ab][:, 0:n], op=ALU.mult),
                 reads=[f"gel{ab}", f"accc{ab}"], writes=["gT"])
            for _ in range(sched.get(j, 0)):
                if nxt:
                    nxt.pop(0)()
        while nxt:
            nxt.pop(0)()
        for t in range(nt):
            ts_ = slice(128 * t, 128 * t + 128)
            for half in range(2):
                for j in range(NPAIR):
                    P.op("pe", lambda pe: pe.matmul(banks[4 + half][:, :], lhsT=gT[:, j, ts_], rhs=wdownb[:, j, 512 * half:512 * half + 512],
                                                    start=(j == 0), stop=(j == NPAIR - 1)), reads=["gT", "wdownb"], writes=[B[4 + half]])
            ob_ = ost[otile % 2]
            obR = f"ost{otile % 2}"
            otile += 1
            ln_tile(xb, t, 1, ob_)
            r0 = 128 * (tile0 + t)
            P.dma("pool", lambda g: g.dma_start(out=out[r0:r0 + 128, :], in_=ob_[:]), sem("outs"), reads=[obR], writes=["OUT"])
    P.final_wait("sp", ["OUT"])

    def emit(eng_name, eng):
        for waits, fn, semname, inc in P.ops[eng_name]:
            for k, v in waits:
                eng.wait_ge(sems[k], v)
            if fn is None:
                continue
            name, a, k = fn
            ins = getattr(eng, name)(*a, **k)
            ins.then_inc(sems[semname], inc)

    with nc.Block() as block:
        @block.tensor
        def _(pe):
            emit("pe", pe)

        @block.scalar
        def _(a):
            emit("act", a)

        @block.vector
        def _(v):
            emit("dve", v)

        @block.gpsimd
        def _(g):
            emit("pool", g)

        @block.sync
        def _(q):
            emit("sp", q)
    s3.close()
    sW.close()
    es.close()
    return nc


def _consts():
    c = np.zeros((128, K_END), np.float32)
    idx = np.arange(128)
    s_, t_ = idx[:, None], idx[None, :]
    same = (s_ // 64) == (t_ // 64)
    c[:, K_ID:K_ID + 128] = np.eye(128)
    c[:, K_NTRI:K_NTRI + 128] = -1.0 * (s_ >= t_)
    c[:, K_NONE:K_NONE + 128] = -1.0
    c[:, K_MASKL:K_MASKL + 128] = (s_ < t_)
    ref = 64 * (t_ // 64) + 31
    m1 = np.where(t_ >= ref, ((s_ > ref) & (s_ <= t_)).astype(np.float32), -((s_ > t_) & (s_ <= ref)).astype(np.float32))
    c[:, K_M1:K_M1 + 128] = m1 * same
    c[:, K_M3:K_M3 + 128] = same & (s_ <= t_)
    c[:, K_IND:K_IND + 2] = (idx[:, None] // 64) == np.arange(2)[None, :]
    c[:, K_M2:K_M2 + 128] = same & (s_ > t_)
    c[:, K_ONES:K_ONES + 128] = 1.0
    c[:, K_MEAN:K_MEAN + 128] = 1.0 / 128
    c[:, K_MASKL2:K_MASKL2 + 128] = (s_ < t_)
    c[:, K_MASKL2 + 128:K_MASKL2 + 256] = (s_ < t_)
    return c


_NC_CACHE = {}


def kernel(x, w_in, gate_up, gate_bias, gla_norm_g, w_out, ln1_g, ln1_b, w_up, conv_w, conv_b, w_down, ln2_g, ln2_b):
    f = lambda a: np.ascontiguousarray(np.asarray(a, dtype=np.float32))
    x = f(x); w_in = f(w_in)[0]; gate_up = f(gate_up)[0]; gate_bias = f(gate_bias)[0]; gla_norm_g = f(gla_norm_g)[0]
    w_out = f(w_out)[0]; w_up = f(w_up)[0]; conv_w = f(conv_w)[0]; conv_b = f(conv_b)[0]; w_down = f(w_down)[0]
    lnp = np.stack([f(ln1_g)[0], f(ln1_b)[0], f(ln2_g)[0], f(ln2_b)[0]], 0)
    if "nc" not in _NC_CACHE:
        _NC_CACHE["nc"] = build()
    nc = _NC_CACHE["nc"]
    cst = _consts()
    perm = np.concatenate([np.arange(128) + (256 * r + 128 * c if c < 2 else 512 + 256 * r + 128 * (c - 2))
                           for c in range(4) for r in range(2)])
    wout_r = f(w_out[perm].reshape(8, 128, D).transpose(1, 0, 2))
    wup_r = f(w_up.reshape(8, 128, 2, NPAIR, 128).transpose(3, 1, 2, 0, 4))
    convp = np.concatenate([conv_w, conv_b[None, :]], 0)
    convp_r = f(convp.reshape(4, 44, 128).transpose(2, 1, 0))
    wdown_r = f(w_down.reshape(NPAIR, 128, D).transpose(1, 0, 2))
    gng = f(gla_norm_g.reshape(128, 1))
    in_maps = []
    for c in range(NCORE):
        b, g = divmod(c, 2)
        xT = f(x[b].T.reshape(8, 128, 8, 512).transpose(2, 1, 0, 3))
        xo = np.zeros((17 * 128, D), np.float32)
        xo[128:] = x[b, 2048 * g:2048 * g + 2048]
        if g == 1:
            xo[:128] = x[b, 1920:2048]
        cols = np.concatenate([
            np.arange(256 * g, 256 * g + 256), 512 + np.arange(256 * g, 256 * g + 256),
            1536 + np.arange(128 * g, 128 * g + 128), 1792 + np.arange(128 * g, 128 * g + 128),
            2560 + np.arange(256 * g, 256 * g + 256), 3072 + np.arange(16),
            2048 + np.arange(256 * g, 256 * g + 256)])
        w1f = np.zeros((D, W1C), np.float32)
        w1f[:, :C_V] = w_in[:, cols[:C_V]]
        for hl in range(4):
            hh = hl % 2
            w1f[:, C_V + 128 * hl + 64 * hh:C_V + 128 * hl + 64 * hh + 64] = w_in[:, 1024 + 256 * g + 64 * hl:1024 + 256 * g + 64 * hl + 64]
        w1f[:, C_GV:] = w_in[:, cols[C_V:]]
        w1 = f(w1f.reshape(8, 128, W1C).transpose(1, 0, 2))
        fl = np.zeros((128, 2), np.float32)
        fl[:, 0] = g
        fl[:, 1] = 1 - g
        in_maps.append({
            "xT": xT, "xo": xo, "w1": w1, "gup": f(gate_up[:, 128 * g:128 * g + 128]),
            "gbias": f(gate_bias[None, 128 * g:128 * g + 128]), "gng": gng, "wout": wout_r, "lnp": lnp,
            "wup": wup_r, "convp": convp_r, "wdown": wdown_r, "flag": fl, "cst": cst})
    res = run_bass_kernel_spmd(nc, in_maps, core_ids=list(range(NCORE)))
    outp = np.zeros((4, S, D), np.float32)
    for c in range(NCORE):
        b, g = divmod(c, 2)
        outp[b, 2048 * g:2048 * g + 2048] = res.results[c]["out"]
    return outp
```
